# Optimizing a Trainium2 kernel written in Bass

```python
import jax, jax.numpy as jnp
from jax import lax
import numpy as np

D_MODEL = 2048
BATCH = 16
SEQ = 2048
DEPTH = 2
DEC_BATCH = 8
DEC_SEQ = 16
PAST_LEN = 2048

CHUNK = 64
S5_WIDTH = D_MODEL // 2
S5_GROUP = 16
S5_GROUPS = S5_WIDTH // S5_GROUP
S5_STATE = 64
HG_WIDTH = D_MODEL // 2
HG_HEAD_DIM = 128
HG_HEADS = HG_WIDTH // HG_HEAD_DIM
HG_BLOCK = CHUNK // 4
D_FF = ((8 * D_MODEL // 3 + 255) // 256) * 256
N_IN = S5_WIDTH + 4 * HG_WIDTH + 2 * D_MODEL
IN_SPLITS = (S5_WIDTH,
             S5_WIDTH + HG_WIDTH,
             S5_WIDTH + 2 * HG_WIDTH,
             S5_WIDTH + 3 * HG_WIDTH,
             S5_WIDTH + 4 * HG_WIDTH,
             S5_WIDTH + 4 * HG_WIDTH + D_MODEL)
EPS = 1e-6
LB_FLOOR = 1e-30
STEP_MIN = 1e-3
STEP_MAX = 1e-1

kernel_name = 'hybrid_s5_hgrn2_macaron_stream_step'


def rms_norm(x, g):
    xf = x.astype(jnp.float32)
    y = xf * lax.rsqrt(jnp.mean(xf * xf, axis=-1, keepdims=True) + EPS)
    return (y * g.astype(jnp.float32)).astype(x.dtype)


def half_swiglu_ffn(x, g, w13, w2):
    hid = rms_norm(x, g) @ w13
    a, b = jnp.split(hid, 2, axis=-1)
    return (jax.nn.silu(a) * b) @ w2


def _complex_affine_combine(e1, e2):
    a1r, a1i, b1r, b1i = e1
    a2r, a2i, b2r, b2i = e2
    return (a2r * a1r - a2i * a1i,
            a2r * a1i + a2i * a1r,
            a2r * b1r - a2i * b1i + b2r,
            a2r * b1i + a2i * b1r + b2i)


def s5_mixer(u, lam_re, lam_im, log_step, b_re, b_im, c_re, c_im, d_skip, x0_re, x0_im):
    f32 = jnp.float32
    Bn, L, _ = u.shape
    uf = u.astype(f32)
    lr = lam_re.astype(f32)
    li = lam_im.astype(f32)
    dt = jnp.exp(log_step.astype(f32))[:, None]
    mag = jnp.exp(lr * dt)
    a_re = mag * jnp.cos(li * dt)
    a_im = mag * jnp.sin(li * dt)
    den = lr * lr + li * li
    nr = a_re - 1.0
    coef_re = (nr * lr + a_im * li) / den
    coef_im = (a_im * lr - nr * li) / den
    br = b_re.astype(f32)
    bi = b_im.astype(f32)
    bb_re = coef_re[..., None] * br - coef_im[..., None] * bi
    bb_im = coef_re[..., None] * bi + coef_im[..., None] * br
    ug = uf.reshape(Bn, L, S5_GROUPS, S5_GROUP)
    bu_re = jnp.einsum('blgh,gph->lbgp', ug, bb_re)
    bu_im = jnp.einsum('blgh,gph->lbgp', ug, bb_im)
    x0r = x0_re.astype(f32)
    x0i = x0_im.astype(f32)
    bu_re = bu_re.at[0].add(a_re * x0r - a_im * x0i)
    bu_im = bu_im.at[0].add(a_re * x0i + a_im * x0r)
    a_re_t = jnp.broadcast_to(a_re, (L, 1, S5_GROUPS, S5_STATE))
    a_im_t = jnp.broadcast_to(a_im, (L, 1, S5_GROUPS, S5_STATE))
    _, _, xr, xi = lax.associative_scan(_complex_affine_combine,
                                        (a_re_t, a_im_t, bu_re, bu_im), axis=0)
    cr = c_re.astype(f32)
    ci = c_im.astype(f32)
    y = jnp.einsum('lbgp,ghp->blgh', xr, cr) - jnp.einsum('lbgp,ghp->blgh', xi, ci)
    y = y.reshape(Bn, L, S5_WIDTH) + d_skip.astype(f32) * uf
    return y, xr[-1], xi[-1]


def hgrn2_chunkwise(q, k, v, logf, s0):
    Bn, L = q.shape[0], q.shape[1]
    n_blk = -(-L // HG_BLOCK)
    pad = n_blk * HG_BLOCK - L

    def to_blocks(a):
        a = jnp.pad(a, ((0, 0), (0, pad), (0, 0), (0, 0)))
        return a.reshape(Bn, n_blk, HG_BLOCK, HG_HEADS, a.shape[-1]).transpose(1, 0, 3, 2, 4)

    mask = jnp.tril(jnp.ones((HG_BLOCK, HG_BLOCK), dtype=bool))[None, None, :, :, None]

    def step(S, inp):
        qb, kb, vb, gb = inp
        b = jnp.cumsum(gb, axis=2)
        b_last = b[:, :, -1:, :]
        o_inter = jnp.einsum('bhtk,bhkv->bhtv', qb * jnp.exp(b), S)
        diff = b[:, :, :, None, :] - b[:, :, None, :, :]
        decay = jnp.where(mask, jnp.exp(jnp.where(mask, diff, 0.0)), 0.0)
        A = jnp.einsum('bhtk,bhsk,bhtsk->bhts', qb, kb, decay)
        o = o_inter + jnp.einsum('bhts,bhsv->bhtv', A, vb)
        S_new = (jnp.exp(b_last[:, :, 0, :])[..., None] * S
                 + jnp.einsum('bhsk,bhsv->bhkv', kb * jnp.exp(b_last - b), vb))
        return S_new, o

    S_fin, o_blk = lax.scan(step, s0, (to_blocks(q), to_blocks(k), to_blocks(v), to_blocks(logf)))
    o = o_blk.transpose(1, 0, 3, 2, 4).reshape(Bn, n_blk * HG_BLOCK, HG_HEADS, HG_HEAD_DIM)[:, :L]
    return o, S_fin


def trunk_layer(x, s5_x0_re, s5_x0_im, hg_s0, lb,
                norm_ffn1, ffn1_w13, ffn1_w2, norm_mix, w_in,
                s5_lam_re, s5_lam_im, s5_log_step, s5_b_re, s5_b_im, s5_c_re, s5_c_im,
                s5_d, s5_w_glu, hg_norm, hg_w_out, w_out, norm_ffn2, ffn2_w13, ffn2_w2):
    f32 = jnp.float32
    Bn, L, _ = x.shape
    h = x + 0.5 * half_swiglu_ffn(x, norm_ffn1, ffn1_w13, ffn1_w2)
    u = rms_norm(h, norm_mix)
    z = u @ w_in
    s5_in, hq, hf, hi, hg, gate_a, gate_b = jnp.split(z, IN_SPLITS, axis=-1)
    ys5, s5_re, s5_im = s5_mixer(s5_in, s5_lam_re, s5_lam_im, s5_log_step, s5_b_re, s5_b_im,
                                 s5_c_re, s5_c_im, s5_d, s5_x0_re, s5_x0_im)
    ys5 = jax.nn.gelu(ys5).astype(x.dtype) @ s5_w_glu
    glu_a, glu_b = jnp.split(ys5, 2, axis=-1)
    y_a = glu_a * jax.nn.sigmoid(glu_b)
    lbr = lb.astype(f32).reshape(HG_HEADS, HG_HEAD_DIM)
    q = jax.nn.silu(hq.astype(f32)).reshape(Bn, L, HG_HEADS, HG_HEAD_DIM) * (HG_HEAD_DIM ** -0.5)
    zf = hf.astype(f32).reshape(Bn, L, HG_HEADS, HG_HEAD_DIM)
    log_lb = jnp.log(jnp.maximum(lbr, LB_FLOOR))
    logf = jnp.logaddexp(log_lb, jnp.log1p(-lbr) + jax.nn.log_sigmoid(zf))
    k = (1.0 - lbr) * jax.nn.sigmoid(-zf)
    v = hi.astype(f32).reshape(Bn, L, HG_HEADS, HG_HEAD_DIM)
    o, hg_s = hgrn2_chunkwise(q, k, v, logf, hg_s0.astype(f32))
    o = rms_norm(o, hg_norm) * jax.nn.silu(hg.astype(f32).reshape(Bn, L, HG_HEADS, HG_HEAD_DIM))
    y_b = o.reshape(Bn, L, HG_WIDTH).astype(x.dtype) @ hg_w_out
    m = jax.nn.sigmoid(gate_a) * y_a + jax.nn.sigmoid(gate_b) * y_b
    h = h + m @ w_out
    out = h + 0.5 * half_swiglu_ffn(h, norm_ffn2, ffn2_w13, ffn2_w2)
    return out, s5_re, s5_im, hg_s


def setup_inputs(seed: int = 0) -> dict:
    key = jax.random.key(seed)
    ks = jax.random.split(key, 32)
    f32 = jnp.float32

    def nrm(k, shape, scale):
        return jax.random.normal(k, shape, f32) * scale

    lam_im = (jnp.broadcast_to(jnp.pi * jnp.arange(S5_STATE, dtype=f32), (DEPTH, S5_GROUPS, S5_STATE))
              + nrm(ks[26], (DEPTH, S5_GROUPS, S5_STATE), 0.01))
    return {
        'x_prompt': nrm(ks[0], (BATCH, SEQ, D_MODEL), 1.0),
        'x_sample': nrm(ks[1], (DEC_BATCH, DEC_SEQ, D_MODEL), 1.0),
        'state_s5_re': nrm(ks[2], (DEPTH, DEC_BATCH, S5_GROUPS, S5_STATE), 0.5),
        'state_s5_im': nrm(ks[3], (DEPTH, DEC_BATCH, S5_GROUPS, S5_STATE), 0.5),
        'state_hgrn': nrm(ks[4], (DEPTH, DEC_BATCH, HG_HEADS, HG_HEAD_DIM, HG_HEAD_DIM), 0.5),
        'norm_ffn1': 1.0 + nrm(ks[5], (DEPTH, D_MODEL), 0.02),
        'ffn1_w13': nrm(ks[6], (DEPTH, D_MODEL, 2 * D_FF), D_MODEL ** -0.5),
        'ffn1_w2': nrm(ks[7], (DEPTH, D_FF, D_MODEL), D_FF ** -0.5),
        'norm_mix': 1.0 + nrm(ks[8], (DEPTH, D_MODEL), 0.02),
        'w_in': nrm(ks[9], (DEPTH, D_MODEL, N_IN), D_MODEL ** -0.5),
        's5_lam_re': -0.5 + nrm(ks[10], (DEPTH, S5_GROUPS, S5_STATE), 0.01),
        's5_lam_im': lam_im,
        's5_log_step': jax.random.uniform(ks[11], (DEPTH, S5_GROUPS), f32, np.log(STEP_MIN), np.log(STEP_MAX)),
        's5_b_re': nrm(ks[12], (DEPTH, S5_GROUPS, S5_STATE, S5_GROUP), (2 * S5_GROUP) ** -0.5),
        's5_b_im': nrm(ks[13], (DEPTH, S5_GROUPS, S5_STATE, S5_GROUP), (2 * S5_GROUP) ** -0.5),
        's5_c_re': nrm(ks[14], (DEPTH, S5_GROUPS, S5_GROUP, S5_STATE), (2 * S5_STATE) ** -0.5),
        's5_c_im': nrm(ks[15], (DEPTH, S5_GROUPS, S5_GROUP, S5_STATE), (2 * S5_STATE) ** -0.5),
        's5_d': nrm(ks[16], (DEPTH, S5_WIDTH), 1.0),
        's5_w_glu': nrm(ks[17], (DEPTH, S5_WIDTH, 2 * D_MODEL), S5_WIDTH ** -0.5),
        'hg_lb': nrm(ks[18], (DEPTH, HG_WIDTH), 0.1),
        'hg_norm': 1.0 + nrm(ks[19], (DEPTH, HG_HEAD_DIM), 0.02),
        'hg_w_out': nrm(ks[20], (DEPTH, HG_WIDTH, D_MODEL), HG_WIDTH ** -0.5),
        'w_out': nrm(ks[21], (DEPTH, D_MODEL, D_MODEL), D_MODEL ** -0.5),
        'norm_ffn2': 1.0 + nrm(ks[22], (DEPTH, D_MODEL), 0.02),
        'ffn2_w13': nrm(ks[23], (DEPTH, D_MODEL, 2 * D_FF), D_MODEL ** -0.5),
        'ffn2_w2': nrm(ks[24], (DEPTH, D_FF, D_MODEL), D_FF ** -0.5),
        'norm_final': 1.0 + nrm(ks[25], (D_MODEL,), 0.02),
    }


def reference(x_prompt, x_sample, state_s5_re, state_s5_im, state_hgrn,
              norm_ffn1, ffn1_w13, ffn1_w2, norm_mix, w_in,
              s5_lam_re, s5_lam_im, s5_log_step, s5_b_re, s5_b_im, s5_c_re, s5_c_im,
              s5_d, s5_w_glu, hg_lb, hg_norm, hg_w_out, w_out,
              norm_ffn2, ffn2_w13, ffn2_w2, norm_final):
    f32 = jnp.float32
    lb_sm = jax.nn.softmax(hg_lb.astype(f32), axis=0)
    lb_all = jnp.cumsum(lb_sm, axis=0) - lb_sm[0]

    bp = x_prompt.shape[0]
    xp = x_prompt
    xs = x_sample
    p_re, p_im, p_hg = [], [], []
    s_re, s_im, s_hg = [], [], []
    for l in range(DEPTH):
        w = (norm_ffn1[l], ffn1_w13[l], ffn1_w2[l], norm_mix[l], w_in[l],
             s5_lam_re[l], s5_lam_im[l], s5_log_step[l], s5_b_re[l], s5_b_im[l],
             s5_c_re[l], s5_c_im[l], s5_d[l], s5_w_glu[l], hg_norm[l], hg_w_out[l],
             w_out[l], norm_ffn2[l], ffn2_w13[l], ffn2_w2[l])
        zr = jnp.zeros((bp, S5_GROUPS, S5_STATE), f32)
        zh = jnp.zeros((bp, HG_HEADS, HG_HEAD_DIM, HG_HEAD_DIM), f32)
        xp, r1, i1, h1 = trunk_layer(xp, zr, zr, zh, lb_all[l], *w)
        xs, r2, i2, h2 = trunk_layer(xs, state_s5_re[l], state_s5_im[l], state_hgrn[l], lb_all[l], *w)
        p_re.append(r1); p_im.append(i1); p_hg.append(h1)
        s_re.append(r2); s_im.append(i2); s_hg.append(h2)
    y_prompt = rms_norm(xp, norm_final)
    y_sample = rms_norm(xs, norm_final)
    return (y_prompt, y_sample,
            jnp.stack(p_re, axis=0), jnp.stack(p_im, axis=0), jnp.stack(p_hg, axis=0),
            jnp.stack(s_re, axis=0), jnp.stack(s_im, axis=0), jnp.stack(s_hg, axis=0))
```

```python
import contextlib
import math
import numpy as np
import concourse.bass as bass
import concourse.mybir as mybir
from concourse.bass_utils import run_bass_kernel_spmd

F32 = mybir.dt.float32
BF16 = mybir.dt.bfloat16
AF = mybir.ActivationFunctionType
ALU = mybir.AluOpType

ENGS = ("pe", "act", "dve", "pool", "sp")
NCORES = 8
D = 2048
DC = 16
DFF = 5632
FC = 44
NIN = 9216
EPS = 1e-6
WSPEC = {
    "ffn1_w13": (2048, 11264), "ffn1_w2": (5632, 2048), "w_in": (2048, 9216),
    "s5_w_glu": (1024, 4096), "hg_w_out": (1024, 2048), "w_out": (2048, 2048),
    "ffn2_w13": (2048, 11264), "ffn2_w2": (5632, 2048),
}
WORDER = ["ffn1_w13", "ffn1_w2", "w_in", "s5_w_glu", "hg_w_out", "w_out", "ffn2_w13", "ffn2_w2"]
def C_NF1(l): return l * 48
def C_NMX(l): return l * 48 + 16
def C_NF2(l): return l * 48 + 32
C_NFIN = 96
def C_S5D(l): return 112 + 8 * l
def C_LB(l): return 128 + 8 * l
def C_HGN(l): return 144 + l
NV = 146
GELU_C = 2.0 * math.sqrt(2.0 / math.pi)
USE_POOL = False


class Op:
    __slots__ = ("eng", "fn", "deps", "ms", "semkey", "val", "is_dma")

    def __init__(self, eng, fn, is_dma=False, semkey=None):
        self.eng = eng
        self.fn = fn
        self.deps = []
        self.ms = False
        self.semkey = semkey
        self.val = None
        self.is_dma = is_dma


class Sched:
    def __init__(self):
        self.q = {e: [] for e in ENGS}
        self.last_w = {}
        self.readers = {}
        self.dma_cnt = {}

    def add(self, eng, fn, reads=(), writes=(), dma=None):
        op = Op(eng, fn, is_dma=dma is not None, semkey=dma)
        deps = []
        for k in reads:
            w = self.last_w.get(k)
            if w is not None:
                deps.append(w)
        for k in writes:
            w = self.last_w.get(k)
            if w is not None:
                deps.append(w)
            deps.extend(self.readers.get(k, ()))
        seen = set()
        for d in deps:
            if id(d) in seen or d is op:
                continue
            seen.add(id(d))
            if (not d.is_dma) and (not op.is_dma) and d.eng == "pe" and eng == "pe":
                continue
            op.deps.append(d)
            d.ms = True
        for k in reads:
            self.readers.setdefault(k, []).append(op)
        for k in writes:
            self.last_w[k] = op
            self.readers[k] = []
        if op.is_dma:
            c = self.dma_cnt.get(dma, 0) + 16
            self.dma_cnt[dma] = c
            op.val = c
            op.ms = True
        self.q[eng].append(op)
        return op

    def emit(self, nc, final_keys):
        for e in ENGS:
            c = 0
            for op in self.q[e]:
                if op.is_dma:
                    continue
                if op.ms:
                    c += 1
                    op.val = c
        with contextlib.ExitStack() as es:
            esem = {e: es.enter_context(nc.semaphore("ms_" + e)) for e in ENGS}
            dsem = {k: es.enter_context(nc.semaphore("dm_%d" % i)) for i, k in enumerate(self.dma_cnt)}
            block = es.enter_context(nc.Block())

            def run(e, eng_obj):
                waited = {}
                for op in self.q[e]:
                    need = {}
                    for d in op.deps:
                        key = ("d", d.semkey) if d.is_dma else ("e", d.eng)
                        s = dsem[d.semkey] if d.is_dma else esem[d.eng]
                        if need.get(key, (None, 0))[1] < d.val:
                            need[key] = (s, d.val)
                    for key, (s, v) in need.items():
                        if waited.get(key, 0) >= v:
                            continue
                        eng_obj.wait_ge(s, v)
                        waited[key] = v
                    ins = op.fn(eng_obj)
                    if op.is_dma:
                        ins.then_inc(dsem[op.semkey], 16)
                    elif op.ms:
                        ins.then_inc(esem[e], 1)
                if e == "sp":
                    for k in final_keys:
                        if k in self.dma_cnt:
                            eng_obj.wait_ge(dsem[k], self.dma_cnt[k])

            block.tensor(lambda t: run("pe", t))
            block.scalar(lambda a: run("act", a))
            block.vector(lambda v: run("dve", v))
            block.gpsimd(lambda g: run("pool", g))
            block.sync(lambda s: run("sp", s))


class Buf:
    def __init__(self, ap, off, nbytes):
        self.ap = ap
        self.off = off
        self.nbytes = nbytes

    def keys(self, lo=0, hi=None):
        hi = self.nbytes if hi is None else hi
        a = (self.off + lo) // 2048
        b = (self.off + hi - 1) // 2048
        return [("pg", i) for i in range(a, b + 1)]


class StopBuild(Exception):
    pass


DBG_STOP = [None]


def chk(name):
    if DBG_STOP[0] == name:
        raise StopBuild(name)


def build_program(groups, n_layers=2):
    nc = bass.Bass("TRN2", target_bir_lowering=False)
    S = Sched()
    es = contextlib.ExitStack()

    def din(name, shape, dt=F32):
        return nc.dram_tensor(name, list(shape), dt, kind="ExternalInput").ap()

    def dout(name, shape):
        return nc.dram_tensor(name, list(shape), F32, kind="ExternalOutput").ap()

    def dscr(name, shape, dt):
        return nc.dram_tensor(name, list(shape), dt, kind="Internal").ap()

    def sb(name, shape, dt=F32):
        return es.enter_context(nc.sbuf_tensor("sb_" + name, list(shape), dt))

    xp = din("xp", [2, 2048, D])
    xs = din("xs", [16, D])
    sx0 = din("sx0", [2, 128, 32, 2])
    shg = din("shg", [2, 8, 128, 128])
    vecT_d = din("vecT", [128, NV])
    s5p_d = din("s5p", [2, 128, 32, 67])
    ident_d = din("ident", [128, 128])
    mask_d = din("mask", [128, 128])
    scanm_d = din("scanm", [128, 512])
    blkm_d = din("blkm", [128, 5])
    W = {n: din(n, [2, k, m]) for n, (k, m) in WSPEC.items()}
    Wb = {n: dscr(n + "_bf", [2, k, m], BF16) for n, (k, m) in WSPEC.items()}
    W2s = {n: dscr(n + "_sl", [2, 16, 128, 44 * 128], BF16) for n in WSPEC if n.endswith("_w2")}
    tab_d = dscr("tab_d", [2, 128, 32, 2, 128], F32)
    cblk_d = dscr("cblk_d", [2, 128, 32, 2, 128], BF16)
    bblk_d = dscr("bblk_d", [2, 128, 32, 2, 128], BF16)

    yp = dout("yp", [2, 2048, D])
    ys = dout("ys", [16, D])
    o_pre = dout("o_pre", [2, 2, 32, 128])
    o_pim = dout("o_pim", [2, 2, 32, 128])
    o_phg = dout("o_phg", [2, 2, 8, 128, 128])
    o_sre = dout("o_sre", [2, 32, 128])
    o_sim = dout("o_sim", [2, 32, 128])
    o_shg = dout("o_shg", [2, 8, 128, 128])

    h = sb("h", [128, DC, 512])
    uT = sb("uT", [128, DC, 512], BF16)
    SCRB = 88064
    scr = sb("scr", [128, SCRB // 4])
    NSLOT = 3
    SLOTB = 11264
    slabs = [sb("slab%d" % i, [128, SLOTB // 2], BF16) for i in range(NSLOT)]
    Sf = sb("Sf", [128, 2, 8, 128])
    Sbf = sb("Sbf", [128, 2, 8, 128], BF16)
    car = sb("car", [128, 2, 2, 32])
    ebl = sb("ebl", [128, 8, 16])
    ident = sb("ident", [128, 128])
    identb = sb("identb", [128, 128], BF16)
    onesb = sb("onesb", [128, 128], BF16)
    mask = sb("mask", [128, 128])
    scanm = sb("scanm", [128, 512])
    blkm = sb("blkm", [128, 5])
    vecT = sb("vecT", [128, NV])
    lbv = sb("lbv", [128, 2, 8])
    omlv = sb("omlv", [128, 2, 8])
    s5c = sb("s5c", [128, 2, 8, 32])
    fin = sb("fin", [128, 2, 32])
    fint = sb("fint", [32, 2, 128])
    fa2_t = sb("fa2", [128, 512])

    class TBuf:
        def __init__(self, t, name):
            self.ap = t[:]
            self.name = name

        def keys(self, lo=0, hi=None):
            return [self.name]

    fa2 = TBuf(fa2_t, "fa2")
    sgb2 = sb("sgb2", [128, DC, 512], BF16)

    pbanks = [es.enter_context(nc.psum_tensor("pb%d" % i, [128, 512], F32)) for i in range(7)]
    pbf = es.enter_context(nc.psum_tensor("pbf", [128, 1024], BF16))

    def carve(off, shape, dt):
        n = int(np.prod(shape))
        esz = 4 if dt == F32 else 2
        nbytes = n * esz
        assert off % 4 == 0 and off + nbytes <= SCRB, (off, nbytes)
        ap = scr[:, off // 4:(off + nbytes) // 4]
        if dt != F32:
            ap = ap.bitcast(dt)
        if len(shape) == 2:
            ap = ap.rearrange("p (a b) -> p a b", b=shape[1])
        elif len(shape) == 3:
            ap = ap.rearrange("p (a b c) -> p a b c", b=shape[1], c=shape[2])
        return Buf(ap, off, nbytes)

    gT = carve(0, [FC, 512], BF16)
    xst = [carve(49152 + 8192 * i, [2048], F32) for i in range(2)]
    yfull = carve(0, [DC, 512], F32)
    sqt = [carve(81920 + 1024 * i, [512], BF16) for i in range(2)]
    rs = carve(83968, [512], F32)
    mB = carve(65536, [DC, 512], BF16)
    s5b = carve(0, [8, 512], BF16)
    g5T = carve(0, [8, 512], BF16)
    tabs = [carve(16384 + 4096 * i, [4, 2, 128], F32) for i in range(2)]
    cbs = [carve(24576 + 2048 * i, [4, 2, 128], BF16) for i in range(2)]
    bbs = [carve(28672 + 2048 * i, [4, 2, 128], BF16) for i in range(2)]
    Wr = carve(32768, [4, 512], F32)
    Wi = carve(40960, [4, 512], F32)
    tmpA = carve(49152, [512], F32)
    tmpB = carve(51200, [512], F32)
    tmpC = carve(86016, [512], F32)
    xbB = carve(53248, [4, 2, 512], BF16)
    ysf = carve(61440, [512], F32)
    yt1 = carve(63488, [512], F32)
    sgb = carve(16384, [DC, 512], BF16)
    tmpf = carve(32768, [512], F32)
    onT = carve(0, [8, 512], BF16)
    qe = carve(8192, [8, 512], BF16)
    ke = carve(16384, [8, 512], BF16)
    kl = carve(24576, [8, 512], BF16)
    sg = carve(32768, [8, 512], BF16)
    vtok = carve(40960, [4, 1024], BF16)
    fa = carve(49152, [512], F32)
    lg = carve(51200, [512], F32)
    bcs = carve(53248, [512], F32)
    ebb = carve(55296, [512], F32)
    enb = carve(57344, [512], F32)
    AmT = carve(59392, [4, 128], BF16)
    klm = carve(60416, [4, 4, 128], BF16)
    sqh = carve(64512, [4, 128], BF16)
    rstd = carve(49152, [4, 128], F32)
    otmp = carve(51200, [4, 128], F32)
    AmT2 = carve(53248, [4, 128], BF16)
    klm2 = carve(54272, [4, 4, 128], BF16)
    sqh2 = carve(58368, [4, 128], BF16)
    rstd2 = carve(81920, [4, 128], F32)
    otmp2 = carve(83968, [4, 128], F32)

    def A(eng, fn, r=(), w=()):
        return S.add(eng, fn, reads=list(r), writes=list(w))

    def DMA(eng, fn, key, r=(), w=()):
        return S.add(eng, fn, reads=list(r), writes=list(w), dma=key)

    DMA("sp", lambda e: e.dma_start(out=ident[:], in_=ident_d), "c0", w=["ident"])
    DMA("sp", lambda e: e.dma_start(out=mask[:], in_=mask_d), "c1", w=["mask"])
    DMA("sp", lambda e: e.dma_start(out=scanm[:], in_=scanm_d), "c2", w=["scanm"])
    DMA("sp", lambda e: e.dma_start(out=blkm[:], in_=blkm_d), "c3", w=["blkm"])
    DMA("sp", lambda e: e.dma_start(out=vecT[:], in_=vecT_d), "c4", w=["vecT"])
    A("dve", lambda e: e.tensor_copy(identb[:], ident[:]), r=["ident"], w=["identb"])
    A("dve", lambda e: e.memset(onesb[:], 1.0), w=["onesb"])

    lbt = sb("lbt", [128, 6, 8])
    A("act", lambda e: e.activation(out=lbt[:, 0, :], in_=vecT[:, C_LB(0):C_LB(0) + 8], func=AF.Exp), r=["vecT"], w=["lbt0"])
    A("act", lambda e: e.activation(out=lbt[:, 1, :], in_=vecT[:, C_LB(1):C_LB(1) + 8], func=AF.Exp), r=["vecT"], w=["lbt1"])
    A("dve", lambda e: e.tensor_tensor(out=lbt[:, 2, :], in0=lbt[:, 0, :], in1=lbt[:, 1, :], op=ALU.add), r=["lbt0", "lbt1"], w=["lbt2"])
    A("dve", lambda e: e.reciprocal(lbt[:, 3, :], lbt[:, 2, :]), r=["lbt2"], w=["lbt3"])
    A("dve", lambda e: e.tensor_tensor(out=lbt[:, 4, :], in0=lbt[:, 0, :], in1=lbt[:, 3, :], op=ALU.mult), r=["lbt0", "lbt3"], w=["lbt4"])
    A("dve", lambda e: e.tensor_tensor(out=lbt[:, 5, :], in0=lbt[:, 1, :], in1=lbt[:, 3, :], op=ALU.mult), r=["lbt1", "lbt3"], w=["lbt5"])
    A("dve", lambda e: e.tensor_tensor(out=lbv[:, 0, :], in0=lbt[:, 4, :], in1=lbt[:, 4, :], op=ALU.subtract), r=["lbt4"], w=["lbv0"])
    A("dve", lambda e: e.tensor_tensor(out=lbt[:, 2, :], in0=lbt[:, 4, :], in1=lbt[:, 5, :], op=ALU.add), r=["lbt4", "lbt5", "lbt3"], w=["lbt2"])
    A("dve", lambda e: e.tensor_tensor(out=lbv[:, 1, :], in0=lbt[:, 2, :], in1=lbt[:, 4, :], op=ALU.subtract), r=["lbt2", "lbt4"], w=["lbv1"])
    A("dve", lambda e: e.tensor_scalar(out=omlv[:], in0=lbv[:], scalar1=-1.0, scalar2=1.0, op0=ALU.mult, op1=ALU.add), r=["lbv0", "lbv1"], w=["omlv"])

    cvi = [0]

    def convert_layer(l):
        for n in WORDER:
            k, m = WSPEC[n]
            for c0 in range(0, m, 512):
                i = cvi[0]
                cvi[0] += 1
                DMA("pool", lambda e, n=n, l=l, c0=c0: e.dma_start(out=Wb[n][l][:, c0:c0 + 512], in_=W[n][l][:, c0:c0 + 512]),
                    ("cv", i % 2), w=[("cvslot", i % 2), ("w", n, l, c0 // 512)])
            if n.endswith("_w2"):
                for sl_ in range(16):
                    DMA("pool", lambda e, n=n, l=l, sl_=sl_: e.dma_start(out=W2s[n][l][sl_].rearrange("p (k c) -> p k c", c=128),
                                                                       in_=Wb[n][l][:, sl_ * 128:(sl_ + 1) * 128].rearrange("(k p) c -> p k c", p=128)),
                        ("rl", sl_ % 2), r=[("w", n, l, sl_ // 4)], w=[("rlslot", sl_ % 2), ("w2s", n, l, sl_)])

    s5p = slabs[0][:, 0:4288].bitcast(F32).rearrange("p (a b) -> p a b", b=67)
    sw = sb("sw", [128, 12, 32])
    tabwB = carve(0, [32, 2, 128], F32)
    cbwB = carve(32768, [32, 2, 128], BF16)
    bbwB = carve(49152, [32, 2, 128], BF16)
    tabtmpB = carve(65536, [32, 64], F32)
    bbxB = carve(73728, [2, 32, 32], F32)
    tabw, cbw, bbw, tabtmp, bbx = tabwB.ap, cbwB.ap, bbwB.ap, tabtmpB.ap, bbxB.ap
    KBBX, KBBW, KCBW, KTT = bbxB.keys(), bbwB.keys(), cbwB.keys(), tabtmpB.keys()
    bt = slabs[1][:, 0:2048].bitcast(F32).rearrange("p (a b c) -> p a b c", a=2, b=32)
    TWO_PI = 2.0 * math.pi

    def tt(out, a, b, op, r, w, eng="dve"):
        A(eng, lambda e: e.tensor_tensor(out=out, in0=a, in1=b, op=op), r=r, w=w)

    def ts(out, a, s1, s2, op0, op1, r, w):
        A("dve", lambda e: e.tensor_scalar(out=out, in0=a, scalar1=s1, scalar2=s2, op0=op0, op1=op1), r=r, w=w)

    def s5_setup(l):
        DMA("sp", lambda e: e.dma_start(out=s5p, in_=s5p_d[l]), "c5", w=[("slab", 0)])
        lr = s5p[:, :, 0]
        li = s5p[:, :, 1]
        ls = s5p[:, :, 2]
        k = lambda i: sw[:, i, :]
        SW = ["sw"]
        A("act", lambda e: e.activation(out=k(0), in_=ls, func=AF.Exp), r=[("slab", 0)], w=SW)
        tt(k(1), lr, k(0), ALU.mult, [("slab", 0)] + SW, SW)
        A("act", lambda e: e.activation(out=s5c[:, l, 0, :], in_=k(1), func=AF.Exp), r=SW, w=["s5c"])
        tt(k(2), li, k(0), ALU.mult, [("slab", 0)] + SW, SW)
        swi = sw[:, 3, :].bitcast(mybir.dt.int32)
        ts(k(4), k(2), 1.0 / TWO_PI, None, ALU.mult, ALU.bypass, SW, SW)
        A("dve", lambda e: e.tensor_copy(swi, k(4)), r=SW, w=SW)
        A("dve", lambda e: e.tensor_copy(k(4), swi), r=SW, w=SW)
        A("dve", lambda e: e.scalar_tensor_tensor(out=k(5), in0=k(4), scalar=-TWO_PI, in1=k(2), op0=ALU.mult, op1=ALU.add), r=SW, w=SW)
        for _ in range(2):
            ts(k(6), k(5), math.pi, TWO_PI, ALU.is_gt, ALU.mult, SW, SW)
            tt(k(5), k(5), k(6), ALU.subtract, SW, SW)
            ts(k(6), k(5), -math.pi, TWO_PI, ALU.is_lt, ALU.mult, SW, SW)
            tt(k(5), k(5), k(6), ALU.add, SW, SW)
        A("act", lambda e: e.activation(out=s5c[:, l, 2, :], in_=k(5), func=AF.Sin), r=SW, w=["s5c"])
        ts(k(7), k(5), math.pi / 2, None, ALU.add, ALU.bypass, SW, SW)
        ts(k(6), k(7), math.pi, TWO_PI, ALU.is_gt, ALU.mult, SW, SW)
        tt(k(7), k(7), k(6), ALU.subtract, SW, SW)
        A("act", lambda e: e.activation(out=s5c[:, l, 1, :], in_=k(7), func=AF.Sin), r=SW, w=["s5c"])
        c1 = s5c[:, l, 1, :]
        s1 = s5c[:, l, 2, :]
        am = s5c[:, l, 0, :]
        SC = ["s5c"]
        tt(k(0), am, c1, ALU.mult, SC + SW, SW)
        tt(k(1), am, s1, ALU.mult, SC + SW, SW)
        ts(k(2), k(0), -1.0, None, ALU.add, ALU.bypass, SW, SW)
        tt(k(3), lr, lr, ALU.mult, [("slab", 0)] + SW, SW)
        tt(k(4), li, li, ALU.mult, [("slab", 0)] + SW, SW)
        tt(k(3), k(3), k(4), ALU.add, SW, SW)
        A("dve", lambda e: e.reciprocal(k(3), k(3)), r=SW, w=SW)
        tt(k(4), k(2), lr, ALU.mult, [("slab", 0)] + SW, SW)
        tt(k(5), k(1), li, ALU.mult, [("slab", 0)] + SW, SW)
        tt(k(4), k(4), k(5), ALU.add, SW, SW)
        tt(k(4), k(4), k(3), ALU.mult, SW, SW)
        tt(k(5), k(1), lr, ALU.mult, [("slab", 0)] + SW, SW)
        tt(k(6), k(2), li, ALU.mult, [("slab", 0)] + SW, SW)
        tt(k(5), k(5), k(6), ALU.subtract, SW, SW)
        tt(k(5), k(5), k(3), ALU.mult, SW, SW)
        A("dve", lambda e: e.memset(bbx, 0.0), w=KBBX)
        cre = k(4).unsqueeze(2).broadcast_to([128, 32, 16])
        cim = k(5).unsqueeze(2).broadcast_to([128, 32, 16])
        bre = s5p[:, :, 3:19]
        bim = s5p[:, :, 19:35]
        tt(bt[:, 0], cre, bre, ALU.mult, [("slab", 0)] + SW, [("slab", 1)])
        tt(bt[:, 1], cim, bim, ALU.mult, [("slab", 0)] + SW, [("slab", 1)])
        for g2 in range(2):
            ps_ = slice(64 * g2, 64 * g2 + 64)
            tt(bbx[ps_, 0, :, 16 * g2:16 * g2 + 16], bt[ps_, 0], bt[ps_, 1], ALU.subtract, [("slab", 1)], KBBX)
        tt(bt[:, 0], cre, bim, ALU.mult, [("slab", 0)] + SW + KBBX, [("slab", 1)])
        tt(bt[:, 1], cim, bre, ALU.mult, [("slab", 0)] + SW + KBBX, [("slab", 1)])
        for g2 in range(2):
            ps_ = slice(64 * g2, 64 * g2 + 64)
            tt(bbx[ps_, 1, :, 16 * g2:16 * g2 + 16], bt[ps_, 0], bt[ps_, 1], ALU.add, [("slab", 1)], KBBX)
        for ri in range(2):
            for kt in range(8):
                pbk = pbanks[2 + (kt % 2)]
                A("pe", lambda e, ri=ri, kt=kt, pbk=pbk: e.transpose(pbk[:, 0:128], bbx[:, ri, 4 * kt:4 * kt + 4, :].rearrange("p a b -> p (a b)"), ident[:]),
                  r=KBBX + ["ident"], w=[("pb", 2 + (kt % 2))])
                for jj in range(4):
                    A("act", lambda e, ri=ri, kt=kt, jj=jj, pbk=pbk: e.activation(out=bbw[:, 4 * kt + jj, ri, :], in_=pbk[:, 0:128], func=AF.Copy, scale=blkm[:, jj:jj + 1]),
                      r=[("pb", 2 + (kt % 2)), "blkm"], w=KBBW)
        DMA("sp", lambda e: e.dma_start(out=bblk_d[l], in_=bbw), "c6", r=KBBW, w=[("bblk_d", l)])
        A("dve", lambda e: e.memset(cbw, 0.0), w=KCBW)
        for jj in range(4):
            for g2 in range(2):
                ps_ = slice(64 * g2, 64 * g2 + 64)
                c0 = 32 * jj + 16 * g2
                A("dve", lambda e, jj=jj, ps_=ps_, c0=c0: e.tensor_copy(cbw[ps_, jj::4, 0, c0:c0 + 16], s5p[ps_, jj::4, 35:51]), r=[("slab", 0)], w=KCBW)
                A("dve", lambda e, jj=jj, ps_=ps_, c0=c0: e.tensor_scalar(out=cbw[ps_, jj::4, 1, c0:c0 + 16], in0=s5p[ps_, jj::4, 51:67], scalar1=-1.0, scalar2=None, op0=ALU.mult),
                  r=[("slab", 0)], w=KCBW)
        DMA("sp", lambda e: e.dma_start(out=cblk_d[l], in_=cbw), "c7", r=KCBW, w=[("cblk_d", l)])
        TB = tabwB.keys()
        A("dve", lambda e: e.memset(tabw[:, :, 0, 0:1], 1.0), w=TB)
        A("dve", lambda e: e.memset(tabw[:, :, 1, 0:1], 0.0), w=TB)
        A("dve", lambda e: e.tensor_copy(k(8), c1), r=SC + SW, w=SW)
        A("dve", lambda e: e.tensor_copy(k(9), s1), r=SC + SW, w=SW)
        n = 1
        while n < 128:
            cn = k(8).unsqueeze(2).broadcast_to([128, 32, n])
            sn = k(9).unsqueeze(2).broadcast_to([128, 32, n])
            Tre = tabw[:, :, 0, 0:n]
            Tim = tabw[:, :, 1, 0:n]
            Ore = tabw[:, :, 0, n:2 * n]
            Oim = tabw[:, :, 1, n:2 * n]
            tmpn = tabtmp[:, :, 0:n]
            tt(tmpn, Tim, sn, ALU.mult, TB + SW, KTT)
            tt(Ore, Tre, cn, ALU.mult, TB + SW, TB)
            tt(Ore, Ore, tmpn, ALU.subtract, TB + KTT, TB)
            tt(tmpn, Tim, cn, ALU.mult, TB + SW, KTT)
            tt(Oim, Tre, sn, ALU.mult, TB + SW, TB)
            tt(Oim, Oim, tmpn, ALU.add, TB + KTT, TB)
            tt(k(10), k(8), k(8), ALU.mult, SW, SW)
            tt(k(11), k(9), k(9), ALU.mult, SW, SW)
            tt(k(9), k(8), k(9), ALU.mult, SW, SW)
            ts(k(9), k(9), 2.0, None, ALU.mult, ALU.bypass, SW, SW)
            tt(k(8), k(10), k(11), ALU.subtract, SW, SW)
            n *= 2
        A("dve", lambda e: e.tensor_copy(s5c[:, l, 3, :], k(8)), r=SW, w=SC)
        A("dve", lambda e: e.tensor_copy(s5c[:, l, 4, :], k(9)), r=SW, w=SC)
        DMA("sp", lambda e: e.dma_start(out=tab_d[l], in_=tabw), "c8", r=TB, w=[("tab_d", l)])


    def body():
        chk("consts")
        convert_layer(0)
        chk("conv0")
        for l in range(n_layers):
            s5_setup(l)
            chk("s5setup%d" % l)
        if n_layers > 1:
            convert_layer(1)
        chk("conv1")
        group_loop()

    slot_ctr = [0]

    def load_slab(wname, l, col0, ncols, KC):
        slot = slot_ctr[0] % NSLOT
        slot_ctr[0] += 1
        view = slabs[slot][:, 0:KC * ncols].rearrange("p (k n) -> p k n", n=ncols)
        if wname.endswith("_w2"):
            src2 = W2s[wname][l][col0 // 128]
            DMA("sp", lambda e: e.dma_start(out=slabs[slot][:, 0:KC * ncols], in_=src2), ("slab", slot), r=[("w2s", wname, l, col0 // 128)], w=[("slab", slot)])
            return slot, view
        src = Wb[wname][l][:, col0:col0 + ncols].rearrange("(k p) n -> p k n", p=128)
        rk = sorted({("w", wname, l, c // 512) for c in range(col0, col0 + ncols, 128)})
        DMA("sp", lambda e: e.dma_start(out=view, in_=src), ("slab", slot), r=rk, w=[("slab", slot)])
        return slot, view

    mm_ctr = [0]

    def next_bank():
        b = mm_ctr[0] % 2
        mm_ctr[0] += 1
        return b

    def proj(wname, l, col0, ncols, KC, rhs_fn, rhs_keys, NT, consumer):
        slot, view = load_slab(wname, l, col0, ncols, KC)
        for ci in range(ncols // 128):
            b = next_bank()
            ps = pbanks[b]
            for kc in range(KC):
                A("pe", lambda e, ps=ps, kc=kc, ci=ci: e.matmul(ps[:, :NT], lhsT=view[:, kc, ci * 128:(ci + 1) * 128], rhs=rhs_fn(kc),
                                                                 start=(kc == 0), stop=(kc == KC - 1)),
                  r=[("slab", slot)] + rhs_keys(kc), w=[("pb", b)])
            consumer(col0 // 128 + ci, ps[:, :NT], ("pb", b))

    pend_stats = [None]

    def stats_flush():
        if pend_stats[0] is None:
            return
        c, NT, q = pend_stats[0]
        pend_stats[0] = None
        A("pe", lambda e: e.matmul(pbanks[6][:, :NT], lhsT=onesb[:], rhs=q.ap[:, :NT], start=(c == 0), stop=(c == DC - 1)),
          r=q.keys() + ["onesb"], w=[("pb", 6)])

    def stats_chunk(c, NT):
        stats_flush()
        q = sqt[c % 2]
        A("act", lambda e: e.activation(out=q.ap[:, :NT], in_=h[:, c, :NT], func=AF.Square), r=[("h", c)], w=q.keys())
        pend_stats[0] = (c, NT, q)

    def norm_finish(NT, gcol, out_fn, out_keys):
        stats_flush()
        A("act", lambda e: e.activation(out=rs.ap[:, :NT], in_=pbanks[6][:, :NT], func=AF.Sqrt, scale=1.0 / D, bias=epsc[:, 0:1]), r=[("pb", 6), "epsc"], w=rs.keys())
        A("dve", lambda e: e.reciprocal(rs.ap[:, :NT], rs.ap[:, :NT]), r=rs.keys(), w=rs.keys())

        def one(c):
            A("dve", lambda e: e.scalar_tensor_tensor(out=out_fn(c), in0=h[:, c, :NT], scalar=vecT[:, gcol + c:gcol + c + 1], in1=rs.ap[:, :NT],
                                                      op0=ALU.mult, op1=ALU.mult),
              r=[("h", c), "vecT"] + rs.keys(), w=out_keys(c))
        for c in range(DC):
            one(c)

    epsc = sb("epsc", [128, 2])
    A("dve", lambda e: e.memset(epsc[:, 0:1], EPS), w=["epsc"])
    A("dve", lambda e: e.memset(epsc[:, 1:2], EPS), w=["epsc"])

    def ffn(l, which, NT):
        gcol = C_NF1(l) if which == 1 else C_NF2(l)
        norm_finish(NT, gcol, lambda c: uT[:, c, :NT], lambda c: [("uT", c)])
        w13 = "ffn%d_w13" % which
        w2 = "ffn%d_w2" % which
        rf = lambda kc: uT[:, kc, :NT]
        rk = lambda kc: [("uT", kc)]

        def cons_a(ci, ps, pk):
            A("act", lambda e: e.activation(out=gT.ap[:, ci, :NT], in_=ps, func=AF.Silu), r=[pk], w=gT.keys(ci * 1024, ci * 1024 + 1024))

        def cons_b(ci, ps, pk):
            i = ci - FC
            A("dve", lambda e: e.tensor_tensor(out=gT.ap[:, i, :NT], in0=ps, in1=gT.ap[:, i, :NT], op=ALU.mult),
              r=[pk] + gT.keys(i * 1024, i * 1024 + 1024), w=gT.keys(i * 1024, i * 1024 + 1024))

        for s in range(22):
            proj(w13, l, s * 256, 256, 16, rf, rk, NT, cons_a)
            proj(w13, l, DFF + s * 256, 256, 16, rf, rk, NT, cons_b)

        def cons_o(ci, ps, pk):
            A("dve", lambda e: e.scalar_tensor_tensor(out=h[:, ci, :NT], in0=ps, scalar=0.5, in1=h[:, ci, :NT], op0=ALU.mult, op1=ALU.add),
              r=[pk, ("h", ci)], w=[("h", ci)])
            stats_chunk(ci, NT)

        for s in range(16):
            proj(w2, l, s * 128, 128, FC, lambda kc: gT.ap[:, kc, :NT], lambda kc: gT.keys(kc * 1024, kc * 1024 + 1024), NT, cons_o)

    x0cache = {}
    def mixer(l, NT, gi, first, last, is_sample, seq_slot):
        CL = min(128, NT)
        NCH = NT // CL
        TT = max(1, NT // 128)
        NTT = min(128, NT)
        T = 16 if is_sample else 32
        NB = 1 if is_sample else NTT // T
        LV = 16 if is_sample else NT
        norm_finish(NT, C_NMX(l), lambda c: uT[:, c, :NT], lambda c: [("uT", c)])
        rf = lambda kc: uT[:, kc, :NT]
        rk = lambda kc: [("uT", kc)]

        if first:
            if not is_sample:
                A("dve", lambda e: e.memset(car[:, l], 0.0), w=[("car", l)])
                A("dve", lambda e: e.memset(Sf[:, l], 0.0), w=[("Sf", l, hh) for hh in range(8)])
                A("dve", lambda e: e.memset(Sbf[:, l], 0.0), w=[("Sbf", l, hh) for hh in range(8)])
            else:
                if "x0" not in x0cache:
                    x0cache["x0"] = sb("x0s", [128, 32, 2])
                    x0cache["xt"] = sb("x0ts", [128, 2, 32])
                x0 = x0cache["x0"]
                xt_ = x0cache["xt"]
                DMA("sp", lambda e: e.dma_start(out=x0[:], in_=sx0[l]), ("x0", l), w=["x0s"])
                c1 = s5c[:, l, 1, :]
                s1 = s5c[:, l, 2, :]
                K0 = ["x0s", "s5c"]
                tt(xt_[:, 0], x0[:, :, 0], c1, ALU.mult, K0, ["x0ts"])
                tt(xt_[:, 1], x0[:, :, 1], s1, ALU.mult, K0, ["x0t1s"])
                tt(car[:, l, 0], xt_[:, 0], xt_[:, 1], ALU.subtract, ["x0ts", "x0t1s"], [("car", l)])
                tt(xt_[:, 0], x0[:, :, 0], s1, ALU.mult, K0 + [("car", l)], ["x0ts"])
                tt(xt_[:, 1], x0[:, :, 1], c1, ALU.mult, K0 + [("car", l)], ["x0t1s"])
                tt(car[:, l, 1], xt_[:, 0], xt_[:, 1], ALU.add, ["x0ts", "x0t1s"], [("car", l)])
                DMA("sp", lambda e: e.dma_start(out=Sf[:, l], in_=shg[l].rearrange("h k v -> k h v")), ("shg", l), w=[("Sf", l, hh) for hh in range(8)])
                A("act", lambda e: e.activation(out=Sbf[:, l], in_=Sf[:, l], func=AF.Copy), r=[("Sf", l, hh) for hh in range(8)], w=[("Sbf", l, hh) for hh in range(8)])

        def cons_s5(ci, ps, pk):
            A("act", lambda e: e.activation(out=s5b.ap[:, ci, :NT], in_=ps, func=AF.Copy), r=[pk], w=s5b.keys(ci * 1024, ci * 1024 + 1024))

        for s in range(4):
            proj("w_in", l, s * 256, 256, 16, rf, rk, NT, cons_s5)

        chk('m_s5proj')
        def s5_ctx(tg):
            sl = tg % 2
            return tabs[sl], cbs[sl], bbs[sl], sl

        def s5_A(tg):
            tb, cb, bb_, sl = s5_ctx(tg)
            DMA("sp", lambda e: e.dma_start(out=tb.ap, in_=tab_d[l][:, 4 * tg:4 * tg + 4]), ("tab", sl), r=[("tab_d", l)], w=tb.keys())
            DMA("sp", lambda e: e.dma_start(out=cb.ap, in_=cblk_d[l][:, 4 * tg:4 * tg + 4]), ("cb", sl), r=[("cblk_d", l)], w=cb.keys())
            DMA("sp", lambda e: e.dma_start(out=bb_.ap, in_=bblk_d[l][:, 4 * tg:4 * tg + 4]), ("bb", sl), r=[("bblk_d", l)], w=bb_.keys())

            def tile(jj):
                pa = 2 + 2 * (jj % 2)
                pre, pim = pbanks[pa], pbanks[pa + 1]
                A("pe", lambda e: e.matmul(pre[:, :NT], lhsT=bb_.ap[:, jj, 0, :], rhs=s5b.ap[:, tg, :NT], start=True, stop=True),
                  r=bb_.keys() + s5b.keys(tg * 1024, tg * 1024 + 1024), w=[("pb", pa)])
                A("pe", lambda e: e.matmul(pim[:, :NT], lhsT=bb_.ap[:, jj, 1, :], rhs=s5b.ap[:, tg, :NT], start=True, stop=True),
                  r=bb_.keys() + s5b.keys(tg * 1024, tg * 1024 + 1024), w=[("pb", pa + 1)])
                cT = tb.ap[:, jj, 0, 0:CL].unsqueeze(1).broadcast_to([128, NCH, CL])
                sT = tb.ap[:, jj, 1, 0:CL].unsqueeze(1).broadcast_to([128, NCH, CL])
                v3 = lambda ap: ap.rearrange("p (c t) -> p c t", t=CL)
                wr = v3(Wr.ap[:, jj, :NT])
                wi = v3(Wi.ap[:, jj, :NT])
                ta = v3(tmpA.ap[:, :NT])
                tb_ = v3(tmpB.ap[:, :NT])
                kW = Wr.keys(jj * 2048, jj * 2048 + 2048)
                kWi = Wi.keys(jj * 2048, jj * 2048 + 2048)
                tt(wr, v3(pre[:, :NT]), cT, ALU.mult, [("pb", pa)] + tb.keys(), kW)
                tt(ta, v3(pim[:, :NT]), sT, ALU.mult, [("pb", pa + 1)] + tb.keys(), tmpA.keys())
                tt(wi, v3(pim[:, :NT]), cT, ALU.mult, [("pb", pa + 1)] + tb.keys(), kWi)
                tt(tb_, v3(pre[:, :NT]), sT, ALU.mult, [("pb", pa)] + tb.keys(), tmpB.keys())
                tt(wr, wr, ta, ALU.add, kW + tmpA.keys(), kW)
                tt(wi, wi, tb_, ALU.subtract, kWi + tmpB.keys(), kWi)

            for jj in range(4):
                tile(jj)

        def s5_B(tg, between=()):
            between = list(between)
            def chunk(c):
                for jj in range(4):
                    j = 4 * tg + jj
                    kW = Wr.keys(jj * 2048, jj * 2048 + 2048)
                    kWi = Wi.keys(jj * 2048, jj * 2048 + 2048)
                    am = s5c[:, l, 0, j:j + 1].broadcast_to([128, CL])
                    A("dve", lambda e, jj=jj, j=j, am=am: e.tensor_tensor_scan(out=Wr.ap[:, jj, c * CL:(c + 1) * CL], data0=am, data1=Wr.ap[:, jj, c * CL:(c + 1) * CL],
                                                                            initial=car[:, l, 0, j:j + 1], op0=ALU.mult, op1=ALU.add),
                      r=kW + [("car", l), "s5c"], w=kW)
                    A("dve", lambda e, jj=jj, j=j, am=am: e.tensor_tensor_scan(out=Wi.ap[:, jj, c * CL:(c + 1) * CL], data0=am, data1=Wi.ap[:, jj, c * CL:(c + 1) * CL],
                                                                            initial=car[:, l, 1, j:j + 1], op0=ALU.mult, op1=ALU.add),
                      r=kWi + [("car", l), "s5c"], w=kWi)
                wl_r = Wr.ap[:, :, (c + 1) * CL - 1]
                wl_i = Wi.ap[:, :, (c + 1) * CL - 1]
                cc = s5c[:, l, 3, 4 * tg:4 * tg + 4]
                ss = s5c[:, l, 4, 4 * tg:4 * tg + 4]
                cr_ = car[:, l, 0, 4 * tg:4 * tg + 4]
                ci_ = car[:, l, 1, 4 * tg:4 * tg + 4]
                t4a = sw[:, 10, 0:4]
                t4b = sw[:, 11, 0:4]
                t4e = sw[:, 6, 0:4]
                t4f = sw[:, 7, 0:4]
                KK = Wr.keys() + Wi.keys() + ["s5c"]
                tt(t4a, wl_r, cc, ALU.mult, KK, ["t4a"])
                tt(t4b, wl_i, ss, ALU.mult, KK, ["t4b"])
                tt(t4e, wl_r, ss, ALU.mult, KK, ["t4e"])
                tt(t4f, wl_i, cc, ALU.mult, KK, ["t4f"])
                tt(cr_, t4a, t4b, ALU.subtract, ["t4a", "t4b"], [("car", l)])
                tt(ci_, t4e, t4f, ALU.add, ["t4e", "t4f"], [("car", l)])

            for c in range(NCH):
                chunk(c)
                if between:
                    between.pop(0)()
            while between:
                between.pop(0)()

        def s5_C(tg):
            tb, cb, bb_, sl = s5_ctx(tg)
            PE_ = "dve"
            xb4 = xbB.ap.rearrange("p a b t -> p (a b) t").rearrange("p (s q) t -> p s q t", q=4)

            def tile(jj):
                cT = tb.ap[:, jj, 0, 0:CL].unsqueeze(1).broadcast_to([128, NCH, CL])
                sT = tb.ap[:, jj, 1, 0:CL].unsqueeze(1).broadcast_to([128, NCH, CL])
                v3 = lambda ap: ap.rearrange("p (c t) -> p c t", t=CL)
                wr = v3(Wr.ap[:, jj, :NT])
                wi = v3(Wi.ap[:, jj, :NT])
                kW = Wr.keys(jj * 2048, jj * 2048 + 2048)
                kWi = Wi.keys(jj * 2048, jj * 2048 + 2048)
                s_ = jj % 2
                kx = xbB.keys(s_ * 4096, s_ * 4096 + 4096)
                P = [v3(xb4[:, s_, q, :NT]) for q in range(4)]
                tt(P[0], wr, cT, ALU.mult, kW + tb.keys(), kx, eng=PE_)
                A(PE_, lambda e: e.scalar_tensor_tensor(out=P[1], in0=wi, scalar=-1.0, in1=sT, op0=ALU.mult, op1=ALU.mult), r=kWi + tb.keys(), w=kx)
                tt(P[2], wr, sT, ALU.mult, kW + tb.keys(), kx, eng=PE_)
                tt(P[3], wi, cT, ALU.mult, kWi + tb.keys(), kx, eng=PE_)
                for q in range(4):
                    ri = 0 if q < 2 else 1
                    A("pe", lambda e, q=q, ri=ri: e.matmul(pbanks[6][:, :NT], lhsT=cb.ap[:, jj, ri, :], rhs=xb4[:, s_, q, :NT],
                                                           start=(jj == 0 and q == 0), stop=(jj == 3 and q == 3)),
                      r=cb.keys() + kx, w=[("pb", 6)])

            for jj in range(4):
                tile(jj)
            if last:
                PE2_ = "dve"
                lp = (LV - 1)
                wl_r = Wr.ap[:, :, lp]
                wl_i = Wi.ap[:, :, lp]
                cc = tb.ap[:, :, 0, lp % CL]
                ss = tb.ap[:, :, 1, lp % CL]
                t4a = sw[:, 8, 0:4]
                t4b = sw[:, 9, 0:4]
                KK = Wr.keys() + Wi.keys() + tb.keys()
                tt(t4a, wl_r, cc, ALU.mult, KK, ["t4c"], eng=PE2_)
                tt(t4b, wl_i, ss, ALU.mult, KK, ["t4d"], eng=PE2_)
                tt(fin[:, 0, 4 * tg:4 * tg + 4], t4a, t4b, ALU.subtract, ["t4c", "t4d"], ["fin"], eng=PE2_)
                tt(t4a, wl_r, ss, ALU.mult, KK + ["fin"], ["t4c"], eng=PE2_)
                tt(t4b, wl_i, cc, ALU.mult, KK + ["fin"], ["t4d"], eng=PE2_)
                tt(fin[:, 1, 4 * tg:4 * tg + 4], t4a, t4b, ALU.add, ["t4c", "t4d"], ["fin"], eng=PE2_)

        def s5_Cproj(tg):
            pass

        def s5_G_parts(tg):
            YS = ysf.keys()
            Y1 = yt1.keys()
            EG = "dve"

            def g1():
                A("dve", lambda e: e.scalar_tensor_tensor(out=ysf.ap[:, :NT], in0=s5b.ap[:, tg, :NT], scalar=vecT[:, C_S5D(l) + tg:C_S5D(l) + tg + 1], in1=pbanks[6][:, :NT],
                                                          op0=ALU.mult, op1=ALU.add),
                  r=[("pb", 6), "vecT"] + s5b.keys(tg * 1024, tg * 1024 + 1024), w=YS)
                A("act", lambda e: e.activation(out=yt1.ap[:, :NT], in_=ysf.ap[:, :NT], func=AF.Square), r=YS, w=Y1)

            def g2():
                A(EG, lambda e: e.tensor_scalar(out=yt1.ap[:, :NT], in0=yt1.ap[:, :NT], scalar1=0.044715, scalar2=1.0, op0=ALU.mult, op1=ALU.add), r=Y1, w=Y1)
                tt(yt1.ap[:, :NT], yt1.ap[:, :NT], ysf.ap[:, :NT], ALU.mult, Y1 + YS, Y1, eng=EG)
                A("act", lambda e: e.activation(out=yt1.ap[:, :NT], in_=yt1.ap[:, :NT], func=AF.Sigmoid, scale=GELU_C), r=Y1, w=Y1)

            def g3():
                tt(g5T.ap[:, tg, :NT], ysf.ap[:, :NT], yt1.ap[:, :NT], ALU.mult, Y1 + YS, g5T.keys(tg * 1024, tg * 1024 + 1024), eng=EG)

            return [g1, g2, g3]

        def s5_G(tg):
            for f_ in s5_G_parts(tg):
                f_()

        def cons_ga(ci, ps, pk):
            i = ci - 5120 // 128
            A("act", lambda e: e.activation(out=mB.ap[:, i, :NT], in_=ps, func=AF.Sigmoid), r=[pk], w=mB.keys(i * 1024, i * 1024 + 1024))

        def cons_q(ci, ps, pk):
            hh = ci - 8
            A("act", lambda e: e.activation(out=qe.ap[:, hh, :NT], in_=ps, func=AF.Silu), r=[pk], w=qe.keys(hh * 1024, hh * 1024 + 1024))

        def cons_gb(ci, ps, pk):
            i = ci - 7168 // 128
            A("act", lambda e: e.activation(out=sgb2[:, i, :NT], in_=ps, func=AF.Sigmoid), r=[pk], w=[("sgb2", i)])

        s5_A(0)
        for tg in range(8):
            s5_B(tg, between=(s5_G_parts(tg - 1) if tg >= 1 else ()))
            s5_C(tg)
            if tg + 1 < 8:
                s5_A(tg + 1)
            proj("w_in", l, 5120 + tg * 256, 256, 16, rf, rk, NT, cons_ga)
            if tg % 2 == 1:
                proj("w_in", l, 1024 + (tg // 2) * 256, 256, 16, rf, rk, NT, cons_q)
            proj("w_in", l, 7168 + tg * 256, 256, 16, rf, rk, NT, cons_gb)
            s5_Cproj(tg)
        s5_G(7)
        chk('m_s5')
        if last:
            for ri in range(2):
                A("pe", lambda e, ri=ri: e.transpose(pbanks[2 + ri][0:32, 0:128], fin[:, ri, :], ident[:]), r=["fin", "ident"], w=[("pb", 2 + ri)])
                A("act", lambda e, ri=ri: e.activation(out=fint[:, ri, :], in_=pbanks[2 + ri][0:32, 0:128], func=AF.Copy), r=[("pb", 2 + ri)], w=["fint"])
            if is_sample:
                dre, dim_ = o_sre[l], o_sim[l]
            else:
                dre, dim_ = o_pre[l, seq_slot], o_pim[l, seq_slot]
            DMA("sp", lambda e: e.dma_start(out=dre, in_=fint[:, 0, :]), ("ost", 0), r=["fint"], w=[("ostslot", 0)])
            DMA("sp", lambda e: e.dma_start(out=dim_, in_=fint[:, 1, :]), ("ost", 1), r=["fint"], w=[("ostslot", 1)])

        chk('m_s5fin')
        gf = lambda kc: g5T.ap[:, kc, :NT]
        gk = lambda kc: g5T.keys(kc * 1024, kc * 1024 + 1024)

        def cons_glub(ci, ps, pk):
            i = ci - 16
            A("act", lambda e: e.activation(out=sgb.ap[:, i, :NT], in_=ps, func=AF.Sigmoid), r=[pk], w=sgb.keys(i * 1024, i * 1024 + 1024))

        def cons_glua(ci, ps, pk):
            i = ci
            A("dve", lambda e: e.tensor_tensor(out=tmpf.ap[:, :NT], in0=ps, in1=sgb.ap[:, i, :NT], op=ALU.mult), r=[pk] + sgb.keys(i * 1024, i * 1024 + 1024), w=tmpf.keys())
            A("dve", lambda e: e.tensor_tensor(out=mB.ap[:, i, :NT], in0=tmpf.ap[:, :NT], in1=mB.ap[:, i, :NT], op=ALU.mult),
              r=tmpf.keys() + mB.keys(i * 1024, i * 1024 + 1024), w=mB.keys(i * 1024, i * 1024 + 1024))

        for s in range(4):
            proj("s5_w_glu", l, 2048 + s * 512, 512, 8, gf, gk, NT, cons_glub)
        for s in range(4):
            proj("s5_w_glu", l, s * 512, 512, 8, gf, gk, NT, cons_glua)

        chk('m_glu')
        SCALE = 128.0 ** -0.5
        fa_alt = [fa, fa2]

        def cons_f(ci, ps, pk):
            hh = ci - 16
            fa = fa_alt[hh % 2]
            kq = qe.keys(hh * 1024, hh * 1024 + 1024)
            kk_ = ke.keys(hh * 1024, hh * 1024 + 1024)
            kl_ = kl.keys(hh * 1024, hh * 1024 + 1024)
            lbc = lbv[:, l, hh:hh + 1]
            omc = omlv[:, l, hh:hh + 1]
            A("act", lambda e: e.activation(out=fa.ap[:, :NT], in_=ps, func=AF.Sigmoid), r=[pk], w=fa.keys())
            A("dve", lambda e: e.tensor_scalar(out=fa.ap[:, :NT], in0=fa.ap[:, :NT], scalar1=omc, scalar2=lbc, op0=ALU.mult, op1=ALU.add), r=fa.keys() + ["omlv", "lbv0", "lbv1"], w=fa.keys())
            A("act", lambda e: e.activation(out=lg.ap[:, :NT], in_=fa.ap[:, :NT], func=AF.Ln), r=fa.keys(), w=lg.keys())
            A("dve", lambda e: e.tensor_tensor_scan(out=bcs.ap[:, :NT], data0=scanm[:, :NT], data1=lg.ap[:, :NT], initial=0.0, op0=ALU.mult, op1=ALU.add),
              r=lg.keys() + ["scanm"], w=bcs.keys())
            A("act", lambda e: e.activation(out=ebb.ap[:, :NT], in_=bcs.ap[:, :NT], func=AF.Exp), r=bcs.keys(), w=ebb.keys())
            A("act", lambda e: e.activation(out=enb.ap[:, :NT], in_=bcs.ap[:, :NT], func=AF.Exp, scale=-1.0), r=bcs.keys(), w=enb.keys())
            ts(fa.ap[:, :NT], fa.ap[:, :NT], -1.0, 1.0, ALU.mult, ALU.add, fa.keys() + lg.keys(), fa.keys())
            A("dve", lambda e: e.scalar_tensor_tensor(out=qe.ap[:, hh, :NT], in0=qe.ap[:, hh, :NT], scalar=SCALE, in1=ebb.ap[:, :NT], op0=ALU.mult, op1=ALU.mult),
              r=kq + ebb.keys(), w=kq)
            tt(ke.ap[:, hh, :NT], fa.ap[:, :NT], enb.ap[:, :NT], ALU.mult, fa.keys() + enb.keys(), kk_)
            nblk = NT // T
            A("dve", lambda e: e.tensor_copy(ebl[:, hh, 0:nblk], ebb.ap[:, :NT].rearrange("p (b t) -> p b t", t=T)[:, :, T - 1]), r=ebb.keys(), w=[("ebl", hh)])
            tt(kl.ap[:, hh, :NT].rearrange("p (b t) -> p b t", t=T), ke.ap[:, hh, :NT].rearrange("p (b t) -> p b t", t=T),
               ebl[:, hh, 0:nblk].unsqueeze(2).broadcast_to([128, nblk, T]), ALU.mult, kk_ + [("ebl", hh)], kl_)

        def cons_g(ci, ps, pk):
            hh = ci - 32
            A("act", lambda e: e.activation(out=sg.ap[:, hh, :NT], in_=ps, func=AF.Silu), r=[pk], w=sg.keys(hh * 1024, hh * 1024 + 1024))

        def v_slab(s):
            slot, view = load_slab("w_in", l, 3072 + s * 256, 256, 16)
            for t_ in range(TT):
                b = next_bank()
                ps = pbanks[b]
                for kc in range(16):
                    A("pe", lambda e, ps=ps, kc=kc, t_=t_: e.matmul(ps[:NTT, 0:256], lhsT=uT[:, kc, t_ * 128:t_ * 128 + NTT], rhs=view[:, kc, :], start=(kc == 0), stop=(kc == 15)),
                      r=[("slab", slot), ("uT", kc)], w=[("pb", b)])
                A("act", lambda e, ps=ps, t_=t_, s=s: e.activation(out=vtok.ap[:NTT, t_, s * 256:(s + 1) * 256], in_=ps[:NTT, 0:256], func=AF.Copy),
                  r=[("pb", b)], w=vtok.keys(t_ * 2048, t_ * 2048 + 2048))

        for s in range(4):
            proj("w_in", l, 2048 + s * 256, 256, 16, rf, rk, NT, cons_f)
            proj("w_in", l, 4096 + s * 256, 256, 16, rf, rk, NT, cons_g)
            v_slab(s)
        chk('m_hgprep')
        def rec_gen(t_, hg_):
            tc0 = t_ * 128
            H4 = [4 * hg_ + i for i in range(4)]
            bAT = 2 + hg_
            bO = 4 + hg_
            bU = 6 if hg_ == 0 else 0
            kt0 = hg_ * 512
            AmT_, klm_, sqh_, rstd_, otmp_ = (AmT, klm, sqh, rstd, otmp) if hg_ == 0 else (AmT2, klm2, sqh2, rstd2, otmp2)
            KPBF = ("pbf", hg_)
            for i, hh in enumerate(H4):
                A("pe", lambda e, i=i, hh=hh: e.matmul(pbanks[bAT][:NTT, i * 128:i * 128 + NTT], lhsT=ke.ap[:, hh, tc0:tc0 + NTT], rhs=qe.ap[:, hh, tc0:tc0 + NTT], start=True, stop=True),
                  r=ke.keys(hh * 1024, hh * 1024 + 1024) + qe.keys(hh * 1024, hh * 1024 + 1024), w=[("pb", bAT)])
                A("pe", lambda e, i=i, hh=hh: e.transpose(pbf[:NTT, kt0 + i * 128:kt0 + i * 128 + 128], kl.ap[:, hh, tc0:tc0 + NTT], identb[:]),
                  r=kl.keys(hh * 1024, hh * 1024 + 1024) + ["identb"], w=[KPBF])
            yield
            A("dve", lambda e: e.tensor_tensor(out=AmT_.ap[:NTT, :, :NTT], in0=pbanks[bAT][:NTT, :].rearrange("p (a b) -> p a b", b=128)[:, :, :NTT],
                                               in1=mask[:NTT, :NTT].unsqueeze(1).broadcast_to([NTT, 4, NTT]), op=ALU.mult),
              r=[("pb", bAT), "mask"], w=AmT_.keys())
            for jb in range(NB):
                bcol = 4 if is_sample else jb
                A("act", lambda e, jb=jb, bcol=bcol: e.activation(out=klm_.ap[:NTT, jb].rearrange("p a b -> p (a b)"), in_=pbf[:NTT, kt0:kt0 + 512], func=AF.Copy, scale=blkm[:NTT, bcol:bcol + 1]),
                  r=[KPBF, "blkm"], w=klm_.keys(jb * 1024, jb * 1024 + 1024))
            yield
            for i, hh in enumerate(H4):
                A("pe", lambda e, i=i, hh=hh: e.matmul(pbanks[bO][:, i * 128:i * 128 + NTT], lhsT=vtok.ap[:NTT, t_, hh * 128:(hh + 1) * 128], rhs=AmT_.ap[:NTT, i, :NTT],
                                                       start=(i == 0), stop=False, skip_group_check=True),
                  r=vtok.keys(t_ * 2048, t_ * 2048 + 2048) + AmT_.keys(), w=[("pb", bO)])
            for jb in range(NB):
                blk = t_ * NB + jb
                for i, hh in enumerate(H4):
                    A("pe", lambda e, i=i, hh=hh, jb=jb: e.matmul(pbanks[bO][:, i * 128 + jb * T:i * 128 + (jb + 1) * T], lhsT=Sbf[:, l, hh, :],
                                                                 rhs=qe.ap[:, hh, tc0 + jb * T:tc0 + (jb + 1) * T], start=False, stop=(jb == NB - 1 and i == 3), skip_group_check=True),
                      r=[("Sbf", l, hh)] + qe.keys(hh * 1024, hh * 1024 + 1024), w=[("pb", bO)])
                for i, hh in enumerate(H4):
                    A("pe", lambda e, i=i, hh=hh, jb=jb: e.matmul(pbanks[bU][:, i * 128:(i + 1) * 128], lhsT=klm_.ap[:NTT, jb, i, :], rhs=vtok.ap[:NTT, t_, hh * 128:(hh + 1) * 128],
                                                                 start=True, stop=True),
                      r=klm_.keys(jb * 1024, jb * 1024 + 1024) + vtok.keys(t_ * 2048, t_ * 2048 + 2048), w=[("pb", bU)])
                yield
                for i, hh in enumerate(H4):
                    A("dve", lambda e, i=i, hh=hh, blk=blk: e.scalar_tensor_tensor(out=Sf[:, l, hh, :], in0=Sf[:, l, hh, :], scalar=ebl[:, hh, blk:blk + 1],
                                                                                  in1=pbanks[bU][:, i * 128:(i + 1) * 128], op0=ALU.mult, op1=ALU.add),
                      r=[("Sf", l, hh), ("ebl", hh), ("pb", bU)], w=[("Sf", l, hh)])
                    A("act", lambda e, hh=hh: e.activation(out=Sbf[:, l, hh, :], in_=Sf[:, l, hh, :], func=AF.Copy), r=[("Sf", l, hh)], w=[("Sbf", l, hh)])
                yield
            o4 = pbanks[bO][:, :].rearrange("p (a b) -> p a b", b=128)[:, :, :NTT]
            A("act", lambda e: e.activation(out=sqh_.ap[:, :, :NTT], in_=o4, func=AF.Square), r=[("pb", bO)], w=sqh_.keys())
            for i in range(4):
                A("pe", lambda e, i=i: e.matmul(pbanks[bAT][:, i * 128:i * 128 + NTT], lhsT=onesb[:], rhs=sqh_.ap[:, i, :NTT], start=True, stop=True),
                  r=sqh_.keys() + ["onesb"], w=[("pb", bAT)])
            yield
            n4 = pbanks[bAT][:, :].rearrange("p (a b) -> p a b", b=128)[:, :, :NTT]
            A("act", lambda e: e.activation(out=rstd_.ap[:, :, :NTT], in_=n4, func=AF.Sqrt, scale=1.0 / 128, bias=epsc[:, 0:1]), r=[("pb", bAT), "epsc"], w=rstd_.keys())
            A("dve", lambda e: e.reciprocal(rstd_.ap[:, :, :NTT], rstd_.ap[:, :, :NTT]), r=rstd_.keys(), w=rstd_.keys())
            tt(otmp_.ap[:, :, :NTT], o4, rstd_.ap[:, :, :NTT], ALU.mult, [("pb", bO)] + rstd_.keys(), otmp_.keys())
            A("dve", lambda e: e.scalar_tensor_tensor(out=onT.ap[:, 4 * hg_:4 * hg_ + 4, tc0:tc0 + NTT], in0=otmp_.ap[:, :, :NTT], scalar=vecT[:, C_HGN(l):C_HGN(l) + 1],
                                                      in1=sg.ap[:, 4 * hg_:4 * hg_ + 4, tc0:tc0 + NTT], op0=ALU.mult, op1=ALU.mult),
              r=otmp_.keys() + ["vecT"] + sg.keys(), w=onT.keys(4096 * hg_, 4096 * hg_ + 4096))

        def rec_tile(t_):
            gens = [rec_gen(t_, 0), rec_gen(t_, 1)]
            alive = [True, True]
            while any(alive):
                for gi_ in range(2):
                    if alive[gi_]:
                        try:
                            next(gens[gi_])
                        except StopIteration:
                            alive[gi_] = False

        for t_ in range(TT):
            rec_tile(t_)
        if last:
            dst = o_shg[l] if is_sample else o_phg[l, seq_slot]
            DMA("sp", lambda e: e.dma_start(out=dst.rearrange("h k v -> k h v"), in_=Sf[:, l]), ("ost", 2), r=[("Sf", l, hh) for hh in range(8)], w=[("ostslot", 2)])

        chk('m_hgrec')
        def cons_yb(ci, ps, pk):
            i = ci
            A("dve", lambda e: e.tensor_tensor(out=tmpf.ap[:, :NT], in0=ps, in1=sgb2[:, i, :NT], op=ALU.mult), r=[pk, ("sgb2", i)], w=tmpf.keys())
            A("dve", lambda e: e.tensor_tensor(out=mB.ap[:, i, :NT], in0=tmpf.ap[:, :NT], in1=mB.ap[:, i, :NT], op=ALU.add),
              r=tmpf.keys() + mB.keys(i * 1024, i * 1024 + 1024), w=mB.keys(i * 1024, i * 1024 + 1024))

        for s in range(4):
            proj("hg_w_out", l, s * 512, 512, 8, lambda kc: onT.ap[:, kc, :NT], lambda kc: onT.keys(kc * 1024, kc * 1024 + 1024), NT, cons_yb)

        def cons_wo(ci, ps, pk):
            A("dve", lambda e: e.tensor_tensor(out=h[:, ci, :NT], in0=ps, in1=h[:, ci, :NT], op=ALU.add), r=[pk, ("h", ci)], w=[("h", ci)])
            stats_chunk(ci, NT)

        for s in range(8):
            proj("w_out", l, s * 256, 256, 16, lambda kc: mB.ap[:, kc, :NT], lambda kc: mB.keys(kc * 1024, kc * 1024 + 1024), NT, cons_wo)

    io_ctr = [0]

    def group_loop():
      for gi, g in enumerate(groups):
          do_group(gi, g)

    def do_group(gi, g):
      if True:
          kind, seq_slot, tile_i, first, last = g
          is_sample = kind == "s"
          NT = 128 if is_sample else 512
          TT = max(1, NT // 128)
          NTT = min(128, NT)
          NV_ = 16 if is_sample else 128
          for t_ in range(TT):
              st = xst[io_ctr[0] % 2]
              sk = ("io", io_ctr[0] % 2)
              io_ctr[0] += 1
              src = xs[:, :] if is_sample else xp[seq_slot, tile_i * 512 + t_ * 128:tile_i * 512 + t_ * 128 + 128, :]
              if is_sample:
                  A("dve", lambda e, st=st: e.memset(st.ap[:, :], 0.0), w=st.keys())
              DMA("sp", lambda e, st=st, src=src: e.dma_start(out=st.ap[:NV_, :], in_=src), sk, w=st.keys())
              for c4 in range(4):
                  b = next_bank()
                  ps = pbanks[b]
                  for k_ in range(4):
                      c = 4 * c4 + k_
                      A("pe", lambda e, ps=ps, k_=k_, c=c, st=st: e.transpose(ps[:, k_ * 128:k_ * 128 + NTT], st.ap[:NTT, c * 128:(c + 1) * 128], ident[:NTT, :NTT]),
                        r=st.keys() + ["ident"], w=[("pb", b)])
                  A("act", lambda e, ps=ps, c4=c4, t_=t_: e.activation(out=h[:, 4 * c4:4 * c4 + 4, t_ * 128:t_ * 128 + NTT],
                                                                       in_=ps[:, :].rearrange("p (a b) -> p a b", b=128)[:, :, :NTT], func=AF.Copy),
                    r=[("pb", b)], w=[("h", 4 * c4 + k_) for k_ in range(4)])
          for c in range(DC):
              stats_chunk(c, NT)
          chk("xload")
          for l in range(n_layers):
              ffn(l, 1, NT)
              chk("ffn1_%d" % l)
              mixer(l, NT, gi, first, last, is_sample, seq_slot)
              chk("mixer_%d" % l)
              ffn(l, 2, NT)
              chk("ffn2_%d" % l)
          norm_finish(NT, C_NFIN, lambda c: yfull.ap[:, c, :NT], lambda c: yfull.keys(c * 2048, c * 2048 + 2048))
          for t_ in range(TT):
              st = xst[io_ctr[0] % 2]
              sk = ("io", io_ctr[0] % 2)
              io_ctr[0] += 1
              for c4 in range(4):
                  b = next_bank()
                  ps = pbanks[b]
                  for k_ in range(4):
                      c = 4 * c4 + k_
                      A("pe", lambda e, ps=ps, k_=k_, c=c, t_=t_: e.transpose(ps[:NTT, k_ * 128:(k_ + 1) * 128], yfull.ap[:, c, t_ * 128:t_ * 128 + NTT], ident[:]),
                        r=yfull.keys(c * 2048, c * 2048 + 2048) + ["ident"], w=[("pb", b)])
                  A("act", lambda e, ps=ps, c4=c4, st=st: e.activation(out=st.ap[:NTT, c4 * 512:(c4 + 1) * 512], in_=ps[:NTT, :], func=AF.Copy),
                    r=[("pb", b)], w=st.keys())
              dst = ys[:, :] if is_sample else yp[seq_slot, tile_i * 512 + t_ * 128:tile_i * 512 + t_ * 128 + 128, :]
              DMA("sp", lambda e, st=st, dst=dst: e.dma_start(out=dst, in_=st.ap[:NV_, :]), sk, r=st.keys(), w=[("iodone", sk)])

    try:
        body()
    except StopBuild as ex:
        print("DBG stop at", ex)
    print('sbuf bytes remaining', nc.sbuf_bytes_remaining)
    final_keys = [("io", 0), ("io", 1), ("ost", 0), ("ost", 1), ("ost", 2)]
    S.emit(nc, final_keys)
    es.close()
    return nc


def _groups_full():
    gs = []
    for sq in range(2):
        for t in range(4):
            gs.append(("p", sq, t, t == 0, t == 3))
    gs.append(("s", 0, 0, True, True))
    return gs


def _host_layout(inputs, c):
    f32 = np.float32
    d = {}
    d["xp"] = np.ascontiguousarray(inputs["x_prompt"][2 * c:2 * c + 2]).astype(f32, copy=False)
    d["xs"] = np.ascontiguousarray(inputs["x_sample"][c]).astype(f32, copy=False)
    def sm(a):
        return a.reshape(32, 2, 64).transpose(1, 2, 0).reshape(128, 32)
    sx0 = np.stack([np.stack([sm(inputs["state_s5_re"][l, c]), sm(inputs["state_s5_im"][l, c])], axis=-1) for l in range(2)], axis=0)
    d["sx0"] = np.ascontiguousarray(sx0, dtype=f32)
    d["shg"] = np.ascontiguousarray(inputs["state_hgrn"][:, c], dtype=f32)
    return d


def _shared_layout(inputs):
    f32 = np.float32
    d = {}
    cols = []
    def fm(v):
        return np.asarray(v, dtype=f32).reshape(-1, 128).T
    for l in range(2):
        cols += [fm(inputs["norm_ffn1"][l]), fm(inputs["norm_mix"][l]), fm(inputs["norm_ffn2"][l])]
    cols.append(fm(inputs["norm_final"]))
    for l in range(2):
        cols.append(fm(inputs["s5_d"][l]))
    for l in range(2):
        cols.append(fm(inputs["hg_lb"][l]))
    for l in range(2):
        cols.append(fm(inputs["hg_norm"][l]))
    d["vecT"] = np.ascontiguousarray(np.concatenate(cols, axis=1), dtype=f32)
    assert d["vecT"].shape == (128, NV)
    s5p = np.zeros((2, 128, 32, 67), f32)
    for l in range(2):
        def sm(a):
            return np.asarray(a).reshape(32, 2, 64).transpose(1, 2, 0).reshape(128, 32)
        s5p[l, :, :, 0] = sm(inputs["s5_lam_re"][l])
        s5p[l, :, :, 1] = sm(inputs["s5_lam_im"][l])
        s5p[l, :, :, 2] = sm(np.repeat(np.asarray(inputs["s5_log_step"][l])[:, None], 64, axis=1))
        def smb(a):
            return np.asarray(a).reshape(32, 2, 64, 16).transpose(1, 2, 0, 3).reshape(128, 32, 16)
        s5p[l, :, :, 3:19] = smb(inputs["s5_b_re"][l])
        s5p[l, :, :, 19:35] = smb(inputs["s5_b_im"][l])
        def smc(a):
            return np.asarray(a).reshape(32, 2, 16, 64).transpose(1, 3, 0, 2).reshape(128, 32, 16)
        s5p[l, :, :, 35:51] = smc(inputs["s5_c_re"][l])
        s5p[l, :, :, 51:67] = smc(inputs["s5_c_im"][l])
    d["s5p"] = s5p
    d["ident"] = np.eye(128, dtype=f32)
    s = np.arange(128)[:, None]
    t = np.arange(128)[None, :]
    d["mask"] = ((s <= t) & (s // 32 == t // 32)).astype(f32)
    sc = np.ones((128, 512), f32)
    sc[:, 0::32] = 0.0
    d["scanm"] = sc
    bm = np.zeros((128, 5), f32)
    bm[:, 0:4] = (np.arange(128)[:, None] // 32 == np.arange(4)[None, :])
    bm[0:16, 4] = 1.0
    d["blkm"] = bm
    for n in WSPEC:
        d[n] = np.ascontiguousarray(inputs[n], dtype=f32)
    return d


_PROG = {}


def kernel(**inputs):
    groups = _groups_full()
    key = "full"
    if key not in _PROG:
        _PROG[key] = build_program(groups)
    nc = _PROG[key]
    shared = _shared_layout(inputs)
    in_maps = []
    for c in range(NCORES):
        m = dict(shared)
        m.update(_host_layout(inputs, c))
        in_maps.append(m)
    res = run_bass_kernel_spmd(nc, in_maps, core_ids=list(range(NCORES))).results
    f32 = np.float32
    y_prompt = np.concatenate([r["yp"] for r in res], axis=0).astype(f32, copy=False)
    y_sample = np.stack([r["ys"] for r in res], axis=0).astype(f32, copy=False)
    def s5out(name_p):
        a = np.stack([r[name_p] for r in res], axis=1)
        return np.ascontiguousarray(a.reshape(2, 16, 64, 64), dtype=f32)
    p_re = s5out("o_pre")
    p_im = s5out("o_pim")
    p_hg = np.ascontiguousarray(np.stack([r["o_phg"] for r in res], axis=1).reshape(2, 16, 8, 128, 128), dtype=f32)
    s_re = np.ascontiguousarray(np.stack([r["o_sre"] for r in res], axis=1).reshape(2, 8, 64, 64), dtype=f32)
    s_im = np.ascontiguousarray(np.stack([r["o_sim"] for r in res], axis=1).reshape(2, 8, 64, 64), dtype=f32)
    s_hg = np.ascontiguousarray(np.stack([r["o_shg"] for r in res], axis=1), dtype=f32)
    return (y_prompt, y_sample, p_re, p_im, p_hg, s_re, s_im, s_hg)
```

```python
import contextlib
import math
import numpy as np
import concourse.bass as bass
import concourse.mybir as mybir
from concourse.bass_utils import run_bass_kernel_spmd

F32 = mybir.dt.float32
BF16 = mybir.dt.bfloat16
AF = mybir.ActivationFunctionType
ALU = mybir.AluOpType

ENGS = ("pe", "act", "dve", "pool", "sp")
NCORES = 8
D = 2048
DC = 16
DFF = 5632
FC = 44
NIN = 9216
EPS = 1e-6
WSPEC = {
    "ffn1_w13": (2048, 11264), "ffn1_w2": (5632, 2048), "w_in": (2048, 9216),
    "s5_w_glu": (1024, 4096), "hg_w_out": (1024, 2048), "w_out": (2048, 2048),
    "ffn2_w13": (2048, 11264), "ffn2_w2": (5632, 2048),
}
WORDER = ["ffn1_w13", "ffn1_w2", "w_in", "s5_w_glu", "hg_w_out", "w_out", "ffn2_w13", "ffn2_w2"]
def C_NF1(l): return l * 48
def C_NMX(l): return l * 48 + 16
def C_NF2(l): return l * 48 + 32
C_NFIN = 96
def C_S5D(l): return 112 + 8 * l
def C_LB(l): return 128 + 8 * l
def C_HGN(l): return 144 + l
NV = 146
GELU_C = 2.0 * math.sqrt(2.0 / math.pi)
USE_POOL = False


class Op:
    __slots__ = ("eng", "fn", "deps", "ms", "semkey", "val", "is_dma")

    def __init__(self, eng, fn, is_dma=False, semkey=None):
        self.eng = eng
        self.fn = fn
        self.deps = []
        self.ms = False
        self.semkey = semkey
        self.val = None
        self.is_dma = is_dma


class Sched:
    def __init__(self):
        self.q = {e: [] for e in ENGS}
        self.last_w = {}
        self.readers = {}
        self.dma_cnt = {}

    def add(self, eng, fn, reads=(), writes=(), dma=None):
        op = Op(eng, fn, is_dma=dma is not None, semkey=dma)
        deps = []
        for k in reads:
            w = self.last_w.get(k)
            if w is not None:
                deps.append(w)
        for k in writes:
            w = self.last_w.get(k)
            if w is not None:
                deps.append(w)
            deps.extend(self.readers.get(k, ()))
        seen = set()
        for d in deps:
            if id(d) in seen or d is op:
                continue
            seen.add(id(d))
            if (not d.is_dma) and (not op.is_dma) and d.eng == "pe" and eng == "pe":
                continue
            op.deps.append(d)
            d.ms = True
        for k in reads:
            self.readers.setdefault(k, []).append(op)
        for k in writes:
            self.last_w[k] = op
            self.readers[k] = []
        if op.is_dma:
            c = self.dma_cnt.get(dma, 0) + 16
            self.dma_cnt[dma] = c
            op.val = c
            op.ms = True
        self.q[eng].append(op)
        return op

    def emit(self, nc, final_keys):
        for e in ENGS:
            c = 0
            for op in self.q[e]:
                if op.is_dma:
                    continue
                if op.ms:
                    c += 1
                    op.val = c
        with contextlib.ExitStack() as es:
            esem = {e: es.enter_context(nc.semaphore("ms_" + e)) for e in ENGS}
            dsem = {k: es.enter_context(nc.semaphore("dm_%d" % i)) for i, k in enumerate(self.dma_cnt)}
            block = es.enter_context(nc.Block())

            def run(e, eng_obj):
                waited = {}
                for op in self.q[e]:
                    need = {}
                    for d in op.deps:
                        key = ("d", d.semkey) if d.is_dma else ("e", d.eng)
                        s = dsem[d.semkey] if d.is_dma else esem[d.eng]
                        if need.get(key, (None, 0))[1] < d.val:
                            need[key] = (s, d.val)
                    for key, (s, v) in need.items():
                        if waited.get(key, 0) >= v:
                            continue
                        eng_obj.wait_ge(s, v)
                        waited[key] = v
                    ins = op.fn(eng_obj)
                    if op.is_dma:
                        ins.then_inc(dsem[op.semkey], 16)
                    elif op.ms:
                        ins.then_inc(esem[e], 1)
                if e == "sp":
                    for k in final_keys:
                        if k in self.dma_cnt:
                            eng_obj.wait_ge(dsem[k], self.dma_cnt[k])

            block.tensor(lambda t: run("pe", t))
            block.scalar(lambda a: run("act", a))
            block.vector(lambda v: run("dve", v))
            block.gpsimd(lambda g: run("pool", g))
            block.sync(lambda s: run("sp", s))


class Buf:
    def __init__(self, ap, off, nbytes):
        self.ap = ap
        self.off = off
        self.nbytes = nbytes

    def keys(self, lo=0, hi=None):
        hi = self.nbytes if hi is None else hi
        a = (self.off + lo) // 2048
        b = (self.off + hi - 1) // 2048
        return [("pg", i) for i in range(a, b + 1)]


class StopBuild(Exception):
    pass


DBG_STOP = [None]


def chk(name):
    if DBG_STOP[0] == name:
        raise StopBuild(name)


def build_program(groups, n_layers=2):
    nc = bass.Bass("TRN2", target_bir_lowering=False)
    S = Sched()
    es = contextlib.ExitStack()

    def din(name, shape, dt=F32):
        return nc.dram_tensor(name, list(shape), dt, kind="ExternalInput").ap()

    def dout(name, shape):
        return nc.dram_tensor(name, list(shape), F32, kind="ExternalOutput").ap()

    def dscr(name, shape, dt):
        return nc.dram_tensor(name, list(shape), dt, kind="Internal").ap()

    def sb(name, shape, dt=F32):
        return es.enter_context(nc.sbuf_tensor("sb_" + name, list(shape), dt))

    xp = din("xp", [2, 2048, D])
    xs = din("xs", [16, D])
    sx0 = din("sx0", [2, 128, 32, 2])
    shg = din("shg", [2, 8, 128, 128])
    vecT_d = din("vecT", [128, NV])
    s5p_d = din("s5p", [2, 128, 32, 67])
    ident_d = din("ident", [128, 128])
    mask_d = din("mask", [128, 128])
    scanm_d = din("scanm", [128, 512])
    blkm_d = din("blkm", [128, 5])
    W = {n: din(n, [2, k, m]) for n, (k, m) in WSPEC.items()}
    Wb = {n: dscr(n + "_bf", [2, k, m], BF16) for n, (k, m) in WSPEC.items()}
    W2s = {n: dscr(n + "_sl", [2, 16, 128, 44 * 128], BF16) for n in WSPEC if n.endswith("_w2")}
    tab_d = dscr("tab_d", [2, 128, 32, 2, 128], F32)
    cblk_d = dscr("cblk_d", [2, 128, 32, 2, 128], BF16)
    bblk_d = dscr("bblk_d", [2, 128, 32, 2, 128], BF16)

    yp = dout("yp", [2, 2048, D])
    ys = dout("ys", [16, D])
    o_pre = dout("o_pre", [2, 2, 32, 128])
    o_pim = dout("o_pim", [2, 2, 32, 128])
    o_phg = dout("o_phg", [2, 2, 8, 128, 128])
    o_sre = dout("o_sre", [2, 32, 128])
    o_sim = dout("o_sim", [2, 32, 128])
    o_shg = dout("o_shg", [2, 8, 128, 128])

    h = sb("h", [128, DC, 512])
    uT = sb("uT", [128, DC, 512], BF16)
    SCRB = 88064
    scr = sb("scr", [128, SCRB // 4])
    NSLOT = 3
    SLOTB = 11264
    slabs = [sb("slab%d" % i, [128, SLOTB // 2], BF16) for i in range(NSLOT)]
    Sf = sb("Sf", [128, 2, 8, 128])
    Sbf = sb("Sbf", [128, 2, 8, 128], BF16)
    car = sb("car", [128, 2, 2, 32])
    ebl = sb("ebl", [128, 8, 16])
    ident = sb("ident", [128, 128])
    identb = sb("identb", [128, 128], BF16)
    onesb = sb("onesb", [128, 128], BF16)
    mask = sb("mask", [128, 128])
    scanm = sb("scanm", [128, 512])
    blkm = sb("blkm", [128, 5])
    vecT = sb("vecT", [128, NV])
    lbv = sb("lbv", [128, 2, 8])
    omlv = sb("omlv", [128, 2, 8])
    s5c = sb("s5c", [128, 2, 8, 32])
    fin = sb("fin", [128, 2, 32])
    fint = sb("fint", [32, 2, 128])
    fa2_t = sb("fa2", [128, 512])

    class TBuf:
        def __init__(self, t, name):
            self.ap = t[:]
            self.name = name

        def keys(self, lo=0, hi=None):
            return [self.name]

    fa2 = TBuf(fa2_t, "fa2")
    sgb2 = sb("sgb2", [128, DC, 512], BF16)

    pbanks = [es.enter_context(nc.psum_tensor("pb%d" % i, [128, 512], F32)) for i in range(7)]
    pbf = es.enter_context(nc.psum_tensor("pbf", [128, 1024], BF16))

    def carve(off, shape, dt):
        n = int(np.prod(shape))
        esz = 4 if dt == F32 else 2
        nbytes = n * esz
        assert off % 4 == 0 and off + nbytes <= SCRB, (off, nbytes)
        ap = scr[:, off // 4:(off + nbytes) // 4]
        if dt != F32:
            ap = ap.bitcast(dt)
        if len(shape) == 2:
            ap = ap.rearrange("p (a b) -> p a b", b=shape[1])
        elif len(shape) == 3:
            ap = ap.rearrange("p (a b c) -> p a b c", b=shape[1], c=shape[2])
        return Buf(ap, off, nbytes)

    gT = carve(0, [FC, 512], BF16)
    xst = [carve(49152 + 8192 * i, [2048], F32) for i in range(2)]
    yfull = carve(0, [DC, 512], F32)
    sqt = [carve(81920 + 1024 * i, [512], BF16) for i in range(2)]
    rs = carve(83968, [512], F32)
    mB = carve(65536, [DC, 512], BF16)
    s5b = carve(0, [8, 512], BF16)
    g5T = carve(0, [8, 512], BF16)
    tabs = [carve(16384 + 4096 * i, [4, 2, 128], F32) for i in range(2)]
    cbs = [carve(24576 + 2048 * i, [4, 2, 128], BF16) for i in range(2)]
    bbs = [carve(28672 + 2048 * i, [4, 2, 128], BF16) for i in range(2)]
    Wr = carve(32768, [4, 512], F32)
    Wi = carve(40960, [4, 512], F32)
    tmpA = carve(49152, [512], F32)
    tmpB = carve(51200, [512], F32)
    tmpC = carve(86016, [512], F32)
    xbB = carve(53248, [4, 2, 512], BF16)
    ysf = carve(61440, [512], F32)
    yt1 = carve(63488, [512], F32)
    sgb = carve(16384, [DC, 512], BF16)
    tmpf = carve(32768, [512], F32)
    onT = carve(0, [8, 512], BF16)
    qe = carve(8192, [8, 512], BF16)
    ke = carve(16384, [8, 512], BF16)
    kl = carve(24576, [8, 512], BF16)
    sg = carve(32768, [8, 512], BF16)
    vtok = carve(40960, [4, 1024], BF16)
    fa = carve(49152, [512], F32)
    lg = carve(51200, [512], F32)
    bcs = carve(53248, [512], F32)
    ebb = carve(55296, [512], F32)
    enb = carve(57344, [512], F32)
    AmT = carve(59392, [4, 128], BF16)
    klm = carve(60416, [4, 4, 128], BF16)
    sqh = carve(64512, [4, 128], BF16)
    rstd = carve(49152, [4, 128], F32)
    otmp = carve(51200, [4, 128], F32)
    AmT2 = carve(53248, [4, 128], BF16)
    klm2 = carve(54272, [4, 4, 128], BF16)
    sqh2 = carve(58368, [4, 128], BF16)
    rstd2 = carve(81920, [4, 128], F32)
    otmp2 = carve(83968, [4, 128], F32)

    def A(eng, fn, r=(), w=()):
        return S.add(eng, fn, reads=list(r), writes=list(w))

    def DMA(eng, fn, key, r=(), w=()):
        return S.add(eng, fn, reads=list(r), writes=list(w), dma=key)

    DMA("sp", lambda e: e.dma_start(out=ident[:], in_=ident_d), "c0", w=["ident"])
    DMA("sp", lambda e: e.dma_start(out=mask[:], in_=mask_d), "c1", w=["mask"])
    DMA("sp", lambda e: e.dma_start(out=scanm[:], in_=scanm_d), "c2", w=["scanm"])
    DMA("sp", lambda e: e.dma_start(out=blkm[:], in_=blkm_d), "c3", w=["blkm"])
    DMA("sp", lambda e: e.dma_start(out=vecT[:], in_=vecT_d), "c4", w=["vecT"])
    A("dve", lambda e: e.tensor_copy(identb[:], ident[:]), r=["ident"], w=["identb"])
    A("dve", lambda e: e.memset(onesb[:], 1.0), w=["onesb"])

    lbt = sb("lbt", [128, 6, 8])
    A("act", lambda e: e.activation(out=lbt[:, 0, :], in_=vecT[:, C_LB(0):C_LB(0) + 8], func=AF.Exp), r=["vecT"], w=["lbt0"])
    A("act", lambda e: e.activation(out=lbt[:, 1, :], in_=vecT[:, C_LB(1):C_LB(1) + 8], func=AF.Exp), r=["vecT"], w=["lbt1"])
    A("dve", lambda e: e.tensor_tensor(out=lbt[:, 2, :], in0=lbt[:, 0, :], in1=lbt[:, 1, :], op=ALU.add), r=["lbt0", "lbt1"], w=["lbt2"])
    A("dve", lambda e: e.reciprocal(lbt[:, 3, :], lbt[:, 2, :]), r=["lbt2"], w=["lbt3"])
    A("dve", lambda e: e.tensor_tensor(out=lbt[:, 4, :], in0=lbt[:, 0, :], in1=lbt[:, 3, :], op=ALU.mult), r=["lbt0", "lbt3"], w=["lbt4"])
    A("dve", lambda e: e.tensor_tensor(out=lbt[:, 5, :], in0=lbt[:, 1, :], in1=lbt[:, 3, :], op=ALU.mult), r=["lbt1", "lbt3"], w=["lbt5"])
    A("dve", lambda e: e.tensor_tensor(out=lbv[:, 0, :], in0=lbt[:, 4, :], in1=lbt[:, 4, :], op=ALU.subtract), r=["lbt4"], w=["lbv0"])
    A("dve", lambda e: e.tensor_tensor(out=lbt[:, 2, :], in0=lbt[:, 4, :], in1=lbt[:, 5, :], op=ALU.add), r=["lbt4", "lbt5", "lbt3"], w=["lbt2"])
    A("dve", lambda e: e.tensor_tensor(out=lbv[:, 1, :], in0=lbt[:, 2, :], in1=lbt[:, 4, :], op=ALU.subtract), r=["lbt2", "lbt4"], w=["lbv1"])
    A("dve", lambda e: e.tensor_scalar(out=omlv[:], in0=lbv[:], scalar1=-1.0, scalar2=1.0, op0=ALU.mult, op1=ALU.add), r=["lbv0", "lbv1"], w=["omlv"])

    cvi = [0]

    def convert_layer(l):
        for n in WORDER:
            k, m = WSPEC[n]
            for c0 in range(0, m, 512):
                i = cvi[0]
                cvi[0] += 1
                DMA("pool", lambda e, n=n, l=l, c0=c0: e.dma_start(out=Wb[n][l][:, c0:c0 + 512], in_=W[n][l][:, c0:c0 + 512]),
                    ("cv", i % 2), w=[("cvslot", i % 2), ("w", n, l, c0 // 512)])
            if n.endswith("_w2"):
                for sl_ in range(16):
                    DMA("pool", lambda e, n=n, l=l, sl_=sl_: e.dma_start(out=W2s[n][l][sl_].rearrange("p (k c) -> p k c", c=128),
                                                                       in_=Wb[n][l][:, sl_ * 128:(sl_ + 1) * 128].rearrange("(k p) c -> p k c", p=128)),
                        ("rl", sl_ % 2), r=[("w", n, l, sl_ // 4)], w=[("rlslot", sl_ % 2), ("w2s", n, l, sl_)])

    s5p = slabs[0][:, 0:4288].bitcast(F32).rearrange("p (a b) -> p a b", b=67)
    sw = sb("sw", [128, 12, 32])
    tabwB = carve(0, [32, 2, 128], F32)
    cbwB = carve(32768, [32, 2, 128], BF16)
    bbwB = carve(49152, [32, 2, 128], BF16)
    tabtmpB = carve(65536, [32, 64], F32)
    bbxB = carve(73728, [2, 32, 32], F32)
    tabw, cbw, bbw, tabtmp, bbx = tabwB.ap, cbwB.ap, bbwB.ap, tabtmpB.ap, bbxB.ap
    KBBX, KBBW, KCBW, KTT = bbxB.keys(), bbwB.keys(), cbwB.keys(), tabtmpB.keys()
    bt = slabs[1][:, 0:2048].bitcast(F32).rearrange("p (a b c) -> p a b c", a=2, b=32)
    TWO_PI = 2.0 * math.pi

    def tt(out, a, b, op, r, w, eng="dve"):
        A(eng, lambda e: e.tensor_tensor(out=out, in0=a, in1=b, op=op), r=r, w=w)

    def ts(out, a, s1, s2, op0, op1, r, w):
        A("dve", lambda e: e.tensor_scalar(out=out, in0=a, scalar1=s1, scalar2=s2, op0=op0, op1=op1), r=r, w=w)

    def s5_setup(l):
        DMA("sp", lambda e: e.dma_start(out=s5p, in_=s5p_d[l]), "c5", w=[("slab", 0)])
        lr = s5p[:, :, 0]
        li = s5p[:, :, 1]
        ls = s5p[:, :, 2]
        k = lambda i: sw[:, i, :]
        SW = ["sw"]
        A("act", lambda e: e.activation(out=k(0), in_=ls, func=AF.Exp), r=[("slab", 0)], w=SW)
        tt(k(1), lr, k(0), ALU.mult, [("slab", 0)] + SW, SW)
        A("act", lambda e: e.activation(out=s5c[:, l, 0, :], in_=k(1), func=AF.Exp), r=SW, w=["s5c"])
        tt(k(2), li, k(0), ALU.mult, [("slab", 0)] + SW, SW)
        swi = sw[:, 3, :].bitcast(mybir.dt.int32)
        ts(k(4), k(2), 1.0 / TWO_PI, None, ALU.mult, ALU.bypass, SW, SW)
        A("dve", lambda e: e.tensor_copy(swi, k(4)), r=SW, w=SW)
        A("dve", lambda e: e.tensor_copy(k(4), swi), r=SW, w=SW)
        A("dve", lambda e: e.scalar_tensor_tensor(out=k(5), in0=k(4), scalar=-TWO_PI, in1=k(2), op0=ALU.mult, op1=ALU.add), r=SW, w=SW)
        for _ in range(2):
            ts(k(6), k(5), math.pi, TWO_PI, ALU.is_gt, ALU.mult, SW, SW)
            tt(k(5), k(5), k(6), ALU.subtract, SW, SW)
            ts(k(6), k(5), -math.pi, TWO_PI, ALU.is_lt, ALU.mult, SW, SW)
            tt(k(5), k(5), k(6), ALU.add, SW, SW)
        A("act", lambda e: e.activation(out=s5c[:, l, 2, :], in_=k(5), func=AF.Sin), r=SW, w=["s5c"])
        ts(k(7), k(5), math.pi / 2, None, ALU.add, ALU.bypass, SW, SW)
        ts(k(6), k(7), math.pi, TWO_PI, ALU.is_gt, ALU.mult, SW, SW)
        tt(k(7), k(7), k(6), ALU.subtract, SW, SW)
        A("act", lambda e: e.activation(out=s5c[:, l, 1, :], in_=k(7), func=AF.Sin), r=SW, w=["s5c"])
        c1 = s5c[:, l, 1, :]
        s1 = s5c[:, l, 2, :]
        am = s5c[:, l, 0, :]
        SC = ["s5c"]
        tt(k(0), am, c1, ALU.mult, SC + SW, SW)
        tt(k(1), am, s1, ALU.mult, SC + SW, SW)
        ts(k(2), k(0), -1.0, None, ALU.add, ALU.bypass, SW, SW)
        tt(k(3), lr, lr, ALU.mult, [("slab", 0)] + SW, SW)
        tt(k(4), li, li, ALU.mult, [("slab", 0)] + SW, SW)
        tt(k(3), k(3), k(4), ALU.add, SW, SW)
        A("dve", lambda e: e.reciprocal(k(3), k(3)), r=SW, w=SW)
        tt(k(4), k(2), lr, ALU.mult, [("slab", 0)] + SW, SW)
        tt(k(5), k(1), li, ALU.mult, [("slab", 0)] + SW, SW)
        tt(k(4), k(4), k(5), ALU.add, SW, SW)
        tt(k(4), k(4), k(3), ALU.mult, SW, SW)
        tt(k(5), k(1), lr, ALU.mult, [("slab", 0)] + SW, SW)
        tt(k(6), k(2), li, ALU.mult, [("slab", 0)] + SW, SW)
        tt(k(5), k(5), k(6), ALU.subtract, SW, SW)
        tt(k(5), k(5), k(3), ALU.mult, SW, SW)
        A("dve", lambda e: e.memset(bbx, 0.0), w=KBBX)
        cre = k(4).unsqueeze(2).broadcast_to([128, 32, 16])
        cim = k(5).unsqueeze(2).broadcast_to([128, 32, 16])
        bre = s5p[:, :, 3:19]
        bim = s5p[:, :, 19:35]
        tt(bt[:, 0], cre, bre, ALU.mult, [("slab", 0)] + SW, [("slab", 1)])
        tt(bt[:, 1], cim, bim, ALU.mult, [("slab", 0)] + SW, [("slab", 1)])
        for g2 in range(2):
            ps_ = slice(64 * g2, 64 * g2 + 64)
            tt(bbx[ps_, 0, :, 16 * g2:16 * g2 + 16], bt[ps_, 0], bt[ps_, 1], ALU.subtract, [("slab", 1)], KBBX)
        tt(bt[:, 0], cre, bim, ALU.mult, [("slab", 0)] + SW + KBBX, [("slab", 1)])
        tt(bt[:, 1], cim, bre, ALU.mult, [("slab", 0)] + SW + KBBX, [("slab", 1)])
        for g2 in range(2):
            ps_ = slice(64 * g2, 64 * g2 + 64)
            tt(bbx[ps_, 1, :, 16 * g2:16 * g2 + 16], bt[ps_, 0], bt[ps_, 1], ALU.add, [("slab", 1)], KBBX)
        for ri in range(2):
            for kt in range(8):
                pbk = pbanks[2 + (kt % 2)]
                A("pe", lambda e, ri=ri, kt=kt, pbk=pbk: e.transpose(pbk[:, 0:128], bbx[:, ri, 4 * kt:4 * kt + 4, :].rearrange("p a b -> p (a b)"), ident[:]),
                  r=KBBX + ["ident"], w=[("pb", 2 + (kt % 2))])
                for jj in range(4):
                    A("act", lambda e, ri=ri, kt=kt, jj=jj, pbk=pbk: e.activation(out=bbw[:, 4 * kt + jj, ri, :], in_=pbk[:, 0:128], func=AF.Copy, scale=blkm[:, jj:jj + 1]),
                      r=[("pb", 2 + (kt % 2)), "blkm"], w=KBBW)
        DMA("sp", lambda e: e.dma_start(out=bblk_d[l], in_=bbw), "c6", r=KBBW, w=[("bblk_d", l)])
        A("dve", lambda e: e.memset(cbw, 0.0), w=KCBW)
        for jj in range(4):
            for g2 in range(2):
                ps_ = slice(64 * g2, 64 * g2 + 64)
                c0 = 32 * jj + 16 * g2
                A("dve", lambda e, jj=jj, ps_=ps_, c0=c0: e.tensor_copy(cbw[ps_, jj::4, 0, c0:c0 + 16], s5p[ps_, jj::4, 35:51]), r=[("slab", 0)], w=KCBW)
                A("dve", lambda e, jj=jj, ps_=ps_, c0=c0: e.tensor_scalar(out=cbw[ps_, jj::4, 1, c0:c0 + 16], in0=s5p[ps_, jj::4, 51:67], scalar1=-1.0, scalar2=None, op0=ALU.mult),
                  r=[("slab", 0)], w=KCBW)
        DMA("sp", lambda e: e.dma_start(out=cblk_d[l], in_=cbw), "c7", r=KCBW, w=[("cblk_d", l)])
        TB = tabwB.keys()
        A("dve", lambda e: e.memset(tabw[:, :, 0, 0:1], 1.0), w=TB)
        A("dve", lambda e: e.memset(tabw[:, :, 1, 0:1], 0.0), w=TB)
        A("dve", lambda e: e.tensor_copy(k(8), c1), r=SC + SW, w=SW)
        A("dve", lambda e: e.tensor_copy(k(9), s1), r=SC + SW, w=SW)
        n = 1
        while n < 128:
            cn = k(8).unsqueeze(2).broadcast_to([128, 32, n])
            sn = k(9).unsqueeze(2).broadcast_to([128, 32, n])
            Tre = tabw[:, :, 0, 0:n]
            Tim = tabw[:, :, 1, 0:n]
            Ore = tabw[:, :, 0, n:2 * n]
            Oim = tabw[:, :, 1, n:2 * n]
            tmpn = tabtmp[:, :, 0:n]
            tt(tmpn, Tim, sn, ALU.mult, TB + SW, KTT)
            tt(Ore, Tre, cn, ALU.mult, TB + SW, TB)
            tt(Ore, Ore, tmpn, ALU.subtract, TB + KTT, TB)
            tt(tmpn, Tim, cn, ALU.mult, TB + SW, KTT)
            tt(Oim, Tre, sn, ALU.mult, TB + SW, TB)
            tt(Oim, Oim, tmpn, ALU.add, TB + KTT, TB)
            tt(k(10), k(8), k(8), ALU.mult, SW, SW)
            tt(k(11), k(9), k(9), ALU.mult, SW, SW)
            tt(k(9), k(8), k(9), ALU.mult, SW, SW)
            ts(k(9), k(9), 2.0, None, ALU.mult, ALU.bypass, SW, SW)
            tt(k(8), k(10), k(11), ALU.subtract, SW, SW)
            n *= 2
        A("dve", lambda e: e.tensor_copy(s5c[:, l, 3, :], k(8)), r=SW, w=SC)
        A("dve", lambda e: e.tensor_copy(s5c[:, l, 4, :], k(9)), r=SW, w=SC)
        DMA("sp", lambda e: e.dma_start(out=tab_d[l], in_=tabw), "c8", r=TB, w=[("tab_d", l)])


    def body():
        chk("consts")
        convert_layer(0)
        chk("conv0")
        for l in range(n_layers):
            s5_setup(l)
            chk("s5setup%d" % l)
        if n_layers > 1:
            convert_layer(1)
        chk("conv1")
        group_loop()

    slot_ctr = [0]

    def load_slab(wname, l, col0, ncols, KC):
        slot = slot_ctr[0] % NSLOT
        slot_ctr[0] += 1
        view = slabs[slot][:, 0:KC * ncols].rearrange("p (k n) -> p k n", n=ncols)
        if wname.endswith("_w2"):
            src2 = W2s[wname][l][col0 // 128]
            DMA("sp", lambda e: e.dma_start(out=slabs[slot][:, 0:KC * ncols], in_=src2), ("slab", slot), r=[("w2s", wname, l, col0 // 128)], w=[("slab", slot)])
            return slot, view
        src = Wb[wname][l][:, col0:col0 + ncols].rearrange("(k p) n -> p k n", p=128)
        rk = sorted({("w", wname, l, c // 512) for c in range(col0, col0 + ncols, 128)})
        DMA("sp", lambda e: e.dma_start(out=view, in_=src), ("slab", slot), r=rk, w=[("slab", slot)])
        return slot, view

    mm_ctr = [0]

    def next_bank():
        b = mm_ctr[0] % 2
        mm_ctr[0] += 1
        return b

    def proj(wname, l, col0, ncols, KC, rhs_fn, rhs_keys, NT, consumer):
        slot, view = load_slab(wname, l, col0, ncols, KC)
        for ci in range(ncols // 128):
            b = next_bank()
            ps = pbanks[b]
            for kc in range(KC):
                A("pe", lambda e, ps=ps, kc=kc, ci=ci: e.matmul(ps[:, :NT], lhsT=view[:, kc, ci * 128:(ci + 1) * 128], rhs=rhs_fn(kc),
                                                                 start=(kc == 0), stop=(kc == KC - 1)),
                  r=[("slab", slot)] + rhs_keys(kc), w=[("pb", b)])
            consumer(col0 // 128 + ci, ps[:, :NT], ("pb", b))

    pend_stats = [None]

    def stats_flush():
        if pend_stats[0] is None:
            return
        c, NT, q = pend_stats[0]
        pend_stats[0] = None
        A("pe", lambda e: e.matmul(pbanks[6][:, :NT], lhsT=onesb[:], rhs=q.ap[:, :NT], start=(c == 0), stop=(c == DC - 1)),
          r=q.keys() + ["onesb"], w=[("pb", 6)])

    def stats_chunk(c, NT):
        stats_flush()
        q = sqt[c % 2]
        A("act", lambda e: e.activation(out=q.ap[:, :NT], in_=h[:, c, :NT], func=AF.Square), r=[("h", c)], w=q.keys())
        pend_stats[0] = (c, NT, q)

    def norm_finish(NT, gcol, out_fn, out_keys):
        stats_flush()
        A("act", lambda e: e.activation(out=rs.ap[:, :NT], in_=pbanks[6][:, :NT], func=AF.Sqrt, scale=1.0 / D, bias=epsc[:, 0:1]), r=[("pb", 6), "epsc"], w=rs.keys())
        A("dve", lambda e: e.reciprocal(rs.ap[:, :NT], rs.ap[:, :NT]), r=rs.keys(), w=rs.keys())

        def one(c):
            A("dve", lambda e: e.scalar_tensor_tensor(out=out_fn(c), in0=h[:, c, :NT], scalar=vecT[:, gcol + c:gcol + c + 1], in1=rs.ap[:, :NT],
                                                      op0=ALU.mult, op1=ALU.mult),
              r=[("h", c), "vecT"] + rs.keys(), w=out_keys(c))
        for c in range(DC):
            one(c)

    epsc = sb("epsc", [128, 2])
    A("dve", lambda e: e.memset(epsc[:, 0:1], EPS), w=["epsc"])
    A("dve", lambda e: e.memset(epsc[:, 1:2], EPS), w=["epsc"])

    def ffn(l, which, NT):
        gcol = C_NF1(l) if which == 1 else C_NF2(l)
        norm_finish(NT, gcol, lambda c: uT[:, c, :NT], lambda c: [("uT", c)])
        w13 = "ffn%d_w13" % which
        w2 = "ffn%d_w2" % which
        rf = lambda kc: uT[:, kc, :NT]
        rk = lambda kc: [("uT", kc)]

        def cons_a(ci, ps, pk):
            A("act", lambda e: e.activation(out=gT.ap[:, ci, :NT], in_=ps, func=AF.Silu), r=[pk], w=gT.keys(ci * 1024, ci * 1024 + 1024))

        def cons_b(ci, ps, pk):
            i = ci - FC
            A("dve", lambda e: e.tensor_tensor(out=gT.ap[:, i, :NT], in0=ps, in1=gT.ap[:, i, :NT], op=ALU.mult),
              r=[pk] + gT.keys(i * 1024, i * 1024 + 1024), w=gT.keys(i * 1024, i * 1024 + 1024))

        for s in range(22):
            proj(w13, l, s * 256, 256, 16, rf, rk, NT, cons_a)
            proj(w13, l, DFF + s * 256, 256, 16, rf, rk, NT, cons_b)

        def cons_o(ci, ps, pk):
            A("dve", lambda e: e.scalar_tensor_tensor(out=h[:, ci, :NT], in0=ps, scalar=0.5, in1=h[:, ci, :NT], op0=ALU.mult, op1=ALU.add),
              r=[pk, ("h", ci)], w=[("h", ci)])
            stats_chunk(ci, NT)

        for s in range(16):
            proj(w2, l, s * 128, 128, FC, lambda kc: gT.ap[:, kc, :NT], lambda kc: gT.keys(kc * 1024, kc * 1024 + 1024), NT, cons_o)

    x0cache = {}
    def mixer(l, NT, gi, first, last, is_sample, seq_slot):
        CL = min(128, NT)
        NCH = NT // CL
        TT = max(1, NT // 128)
        NTT = min(128, NT)
        T = 16 if is_sample else 32
        NB = 1 if is_sample else NTT // T
        LV = 16 if is_sample else NT
        norm_finish(NT, C_NMX(l), lambda c: uT[:, c, :NT], lambda c: [("uT", c)])
        rf = lambda kc: uT[:, kc, :NT]
        rk = lambda kc: [("uT", kc)]

        if first:
            if not is_sample:
                A("dve", lambda e: e.memset(car[:, l], 0.0), w=[("car", l)])
                A("dve", lambda e: e.memset(Sf[:, l], 0.0), w=[("Sf", l, hh) for hh in range(8)])
                A("dve", lambda e: e.memset(Sbf[:, l], 0.0), w=[("Sbf", l, hh) for hh in range(8)])
            else:
                if "x0" not in x0cache:
                    x0cache["x0"] = sb("x0s", [128, 32, 2])
                    x0cache["xt"] = sb("x0ts", [128, 2, 32])
                x0 = x0cache["x0"]
                xt_ = x0cache["xt"]
                DMA("sp", lambda e: e.dma_start(out=x0[:], in_=sx0[l]), ("x0", l), w=["x0s"])
                c1 = s5c[:, l, 1, :]
                s1 = s5c[:, l, 2, :]
                K0 = ["x0s", "s5c"]
                tt(xt_[:, 0], x0[:, :, 0], c1, ALU.mult, K0, ["x0ts"])
                tt(xt_[:, 1], x0[:, :, 1], s1, ALU.mult, K0, ["x0t1s"])
                tt(car[:, l, 0], xt_[:, 0], xt_[:, 1], ALU.subtract, ["x0ts", "x0t1s"], [("car", l)])
                tt(xt_[:, 0], x0[:, :, 0], s1, ALU.mult, K0 + [("car", l)], ["x0ts"])
                tt(xt_[:, 1], x0[:, :, 1], c1, ALU.mult, K0 + [("car", l)], ["x0t1s"])
                tt(car[:, l, 1], xt_[:, 0], xt_[:, 1], ALU.add, ["x0ts", "x0t1s"], [("car", l)])
                DMA("sp", lambda e: e.dma_start(out=Sf[:, l], in_=shg[l].rearrange("h k v -> k h v")), ("shg", l), w=[("Sf", l, hh) for hh in range(8)])
                A("act", lambda e: e.activation(out=Sbf[:, l], in_=Sf[:, l], func=AF.Copy), r=[("Sf", l, hh) for hh in range(8)], w=[("Sbf", l, hh) for hh in range(8)])

        def cons_s5(ci, ps, pk):
            A("act", lambda e: e.activation(out=s5b.ap[:, ci, :NT], in_=ps, func=AF.Copy), r=[pk], w=s5b.keys(ci * 1024, ci * 1024 + 1024))

        for s in range(4):
            proj("w_in", l, s * 256, 256, 16, rf, rk, NT, cons_s5)

        chk('m_s5proj')
        def s5_ctx(tg):
            sl = tg % 2
            return tabs[sl], cbs[sl], bbs[sl], sl

        def s5_A(tg):
            tb, cb, bb_, sl = s5_ctx(tg)
            DMA("sp", lambda e: e.dma_start(out=tb.ap, in_=tab_d[l][:, 4 * tg:4 * tg + 4]), ("tab", sl), r=[("tab_d", l)], w=tb.keys())
            DMA("sp", lambda e: e.dma_start(out=cb.ap, in_=cblk_d[l][:, 4 * tg:4 * tg + 4]), ("cb", sl), r=[("cblk_d", l)], w=cb.keys())
            DMA("sp", lambda e: e.dma_start(out=bb_.ap, in_=bblk_d[l][:, 4 * tg:4 * tg + 4]), ("bb", sl), r=[("bblk_d", l)], w=bb_.keys())

            def tile(jj):
                pa = 2 + 2 * (jj % 2)
                pre, pim = pbanks[pa], pbanks[pa + 1]
                A("pe", lambda e: e.matmul(pre[:, :NT], lhsT=bb_.ap[:, jj, 0, :], rhs=s5b.ap[:, tg, :NT], start=True, stop=True),
                  r=bb_.keys() + s5b.keys(tg * 1024, tg * 1024 + 1024), w=[("pb", pa)])
                A("pe", lambda e: e.matmul(pim[:, :NT], lhsT=bb_.ap[:, jj, 1, :], rhs=s5b.ap[:, tg, :NT], start=True, stop=True),
                  r=bb_.keys() + s5b.keys(tg * 1024, tg * 1024 + 1024), w=[("pb", pa + 1)])
                cT = tb.ap[:, jj, 0, 0:CL].unsqueeze(1).broadcast_to([128, NCH, CL])
                sT = tb.ap[:, jj, 1, 0:CL].unsqueeze(1).broadcast_to([128, NCH, CL])
                v3 = lambda ap: ap.rearrange("p (c t) -> p c t", t=CL)
                wr = v3(Wr.ap[:, jj, :NT])
                wi = v3(Wi.ap[:, jj, :NT])
                ta = v3(tmpA.ap[:, :NT])
                tb_ = v3(tmpB.ap[:, :NT])
                kW = Wr.keys(jj * 2048, jj * 2048 + 2048)
                kWi = Wi.keys(jj * 2048, jj * 2048 + 2048)
                tt(wr, v3(pre[:, :NT]), cT, ALU.mult, [("pb", pa)] + tb.keys(), kW)
                tt(ta, v3(pim[:, :NT]), sT, ALU.mult, [("pb", pa + 1)] + tb.keys(), tmpA.keys())
                tt(wi, v3(pim[:, :NT]), cT, ALU.mult, [("pb", pa + 1)] + tb.keys(), kWi)
                tt(tb_, v3(pre[:, :NT]), sT, ALU.mult, [("pb", pa)] + tb.keys(), tmpB.keys())
                tt(wr, wr, ta, ALU.add, kW + tmpA.keys(), kW)
                tt(wi, wi, tb_, ALU.subtract, kWi + tmpB.keys(), kWi)

            for jj in range(4):
                tile(jj)

        def s5_B(tg):
            def chunk(c):
                for jj in range(4):
                    j = 4 * tg + jj
                    kW = Wr.keys(jj * 2048, jj * 2048 + 2048)
                    kWi = Wi.keys(jj * 2048, jj * 2048 + 2048)
                    am = s5c[:, l, 0, j:j + 1].broadcast_to([128, CL])
                    A("dve", lambda e, jj=jj, j=j, am=am: e.tensor_tensor_scan(out=Wr.ap[:, jj, c * CL:(c + 1) * CL], data0=am, data1=Wr.ap[:, jj, c * CL:(c + 1) * CL],
                                                                            initial=car[:, l, 0, j:j + 1], op0=ALU.mult, op1=ALU.add),
                      r=kW + [("car", l), "s5c"], w=kW)
                    A("dve", lambda e, jj=jj, j=j, am=am: e.tensor_tensor_scan(out=Wi.ap[:, jj, c * CL:(c + 1) * CL], data0=am, data1=Wi.ap[:, jj, c * CL:(c + 1) * CL],
                                                                            initial=car[:, l, 1, j:j + 1], op0=ALU.mult, op1=ALU.add),
                      r=kWi + [("car", l), "s5c"], w=kWi)
                wl_r = Wr.ap[:, :, (c + 1) * CL - 1]
                wl_i = Wi.ap[:, :, (c + 1) * CL - 1]
                cc = s5c[:, l, 3, 4 * tg:4 * tg + 4]
                ss = s5c[:, l, 4, 4 * tg:4 * tg + 4]
                cr_ = car[:, l, 0, 4 * tg:4 * tg + 4]
                ci_ = car[:, l, 1, 4 * tg:4 * tg + 4]
                t4a = sw[:, 10, 0:4]
                t4b = sw[:, 11, 0:4]
                t4e = sw[:, 6, 0:4]
                t4f = sw[:, 7, 0:4]
                KK = Wr.keys() + Wi.keys() + ["s5c"]
                tt(t4a, wl_r, cc, ALU.mult, KK, ["t4a"])
                tt(t4b, wl_i, ss, ALU.mult, KK, ["t4b"])
                tt(t4e, wl_r, ss, ALU.mult, KK, ["t4e"])
                tt(t4f, wl_i, cc, ALU.mult, KK, ["t4f"])
                tt(cr_, t4a, t4b, ALU.subtract, ["t4a", "t4b"], [("car", l)])
                tt(ci_, t4e, t4f, ALU.add, ["t4e", "t4f"], [("car", l)])

            for c in range(NCH):
                chunk(c)

        def s5_C(tg):
            tb, cb, bb_, sl = s5_ctx(tg)
            PE_ = "dve"
            xb4 = xbB.ap.rearrange("p a b t -> p (a b) t").rearrange("p (s q) t -> p s q t", q=4)

            def tile(jj):
                cT = tb.ap[:, jj, 0, 0:CL].unsqueeze(1).broadcast_to([128, NCH, CL])
                sT = tb.ap[:, jj, 1, 0:CL].unsqueeze(1).broadcast_to([128, NCH, CL])
                v3 = lambda ap: ap.rearrange("p (c t) -> p c t", t=CL)
                wr = v3(Wr.ap[:, jj, :NT])
                wi = v3(Wi.ap[:, jj, :NT])
                kW = Wr.keys(jj * 2048, jj * 2048 + 2048)
                kWi = Wi.keys(jj * 2048, jj * 2048 + 2048)
                s_ = jj % 2
                kx = xbB.keys(s_ * 4096, s_ * 4096 + 4096)
                P = [v3(xb4[:, s_, q, :NT]) for q in range(4)]
                tt(P[0], wr, cT, ALU.mult, kW + tb.keys(), kx, eng=PE_)
                A(PE_, lambda e: e.scalar_tensor_tensor(out=P[1], in0=wi, scalar=-1.0, in1=sT, op0=ALU.mult, op1=ALU.mult), r=kWi + tb.keys(), w=kx)
                tt(P[2], wr, sT, ALU.mult, kW + tb.keys(), kx, eng=PE_)
                tt(P[3], wi, cT, ALU.mult, kWi + tb.keys(), kx, eng=PE_)
                for q in range(4):
                    ri = 0 if q < 2 else 1
                    A("pe", lambda e, q=q, ri=ri: e.matmul(pbanks[6][:, :NT], lhsT=cb.ap[:, jj, ri, :], rhs=xb4[:, s_, q, :NT],
                                                           start=(jj == 0 and q == 0), stop=(jj == 3 and q == 3)),
                      r=cb.keys() + kx, w=[("pb", 6)])

            for jj in range(4):
                tile(jj)
            if last:
                PE2_ = "dve"
                lp = (LV - 1)
                wl_r = Wr.ap[:, :, lp]
                wl_i = Wi.ap[:, :, lp]
                cc = tb.ap[:, :, 0, lp % CL]
                ss = tb.ap[:, :, 1, lp % CL]
                t4a = sw[:, 8, 0:4]
                t4b = sw[:, 9, 0:4]
                KK = Wr.keys() + Wi.keys() + tb.keys()
                tt(t4a, wl_r, cc, ALU.mult, KK, ["t4c"], eng=PE2_)
                tt(t4b, wl_i, ss, ALU.mult, KK, ["t4d"], eng=PE2_)
                tt(fin[:, 0, 4 * tg:4 * tg + 4], t4a, t4b, ALU.subtract, ["t4c", "t4d"], ["fin"], eng=PE2_)
                tt(t4a, wl_r, ss, ALU.mult, KK + ["fin"], ["t4c"], eng=PE2_)
                tt(t4b, wl_i, cc, ALU.mult, KK + ["fin"], ["t4d"], eng=PE2_)
                tt(fin[:, 1, 4 * tg:4 * tg + 4], t4a, t4b, ALU.add, ["t4c", "t4d"], ["fin"], eng=PE2_)

        def s5_Cproj(tg):
            pass

        def s5_G(tg):
            YS = ysf.keys()
            Y1 = yt1.keys()
            EG = "dve" if (gi == 0 or not USE_POOL) else "pool"
            A("dve", lambda e: e.scalar_tensor_tensor(out=ysf.ap[:, :NT], in0=s5b.ap[:, tg, :NT], scalar=vecT[:, C_S5D(l) + tg:C_S5D(l) + tg + 1], in1=pbanks[6][:, :NT],
                                                      op0=ALU.mult, op1=ALU.add),
              r=[("pb", 6), "vecT"] + s5b.keys(tg * 1024, tg * 1024 + 1024), w=YS)
            A("act", lambda e: e.activation(out=yt1.ap[:, :NT], in_=ysf.ap[:, :NT], func=AF.Square), r=YS, w=Y1)
            A(EG, lambda e: e.tensor_scalar(out=yt1.ap[:, :NT], in0=yt1.ap[:, :NT], scalar1=0.044715, scalar2=1.0, op0=ALU.mult, op1=ALU.add), r=Y1, w=Y1)
            tt(yt1.ap[:, :NT], yt1.ap[:, :NT], ysf.ap[:, :NT], ALU.mult, Y1 + YS, Y1, eng=EG)
            A("act", lambda e: e.activation(out=yt1.ap[:, :NT], in_=yt1.ap[:, :NT], func=AF.Sigmoid, scale=GELU_C), r=Y1, w=Y1)
            tt(g5T.ap[:, tg, :NT], ysf.ap[:, :NT], yt1.ap[:, :NT], ALU.mult, Y1 + YS, g5T.keys(tg * 1024, tg * 1024 + 1024), eng=EG)

        def cons_ga(ci, ps, pk):
            i = ci - 5120 // 128
            A("act", lambda e: e.activation(out=mB.ap[:, i, :NT], in_=ps, func=AF.Sigmoid), r=[pk], w=mB.keys(i * 1024, i * 1024 + 1024))

        def cons_q(ci, ps, pk):
            hh = ci - 8
            A("act", lambda e: e.activation(out=qe.ap[:, hh, :NT], in_=ps, func=AF.Silu), r=[pk], w=qe.keys(hh * 1024, hh * 1024 + 1024))

        def cons_gb(ci, ps, pk):
            i = ci - 7168 // 128
            A("act", lambda e: e.activation(out=sgb2[:, i, :NT], in_=ps, func=AF.Sigmoid), r=[pk], w=[("sgb2", i)])

        s5_A(0)
        for tg in range(8):
            s5_B(tg)
            if tg >= 1:
                s5_G(tg - 1)
            s5_C(tg)
            if tg + 1 < 8:
                s5_A(tg + 1)
            proj("w_in", l, 5120 + tg * 256, 256, 16, rf, rk, NT, cons_ga)
            if tg % 2 == 1:
                proj("w_in", l, 1024 + (tg // 2) * 256, 256, 16, rf, rk, NT, cons_q)
            proj("w_in", l, 7168 + tg * 256, 256, 16, rf, rk, NT, cons_gb)
            s5_Cproj(tg)
        s5_G(7)
        chk('m_s5')
        if last:
            for ri in range(2):
                A("pe", lambda e, ri=ri: e.transpose(pbanks[2 + ri][0:32, 0:128], fin[:, ri, :], ident[:]), r=["fin", "ident"], w=[("pb", 2 + ri)])
                A("act", lambda e, ri=ri: e.activation(out=fint[:, ri, :], in_=pbanks[2 + ri][0:32, 0:128], func=AF.Copy), r=[("pb", 2 + ri)], w=["fint"])
            if is_sample:
                dre, dim_ = o_sre[l], o_sim[l]
            else:
                dre, dim_ = o_pre[l, seq_slot], o_pim[l, seq_slot]
            DMA("sp", lambda e: e.dma_start(out=dre, in_=fint[:, 0, :]), ("ost", 0), r=["fint"], w=[("ostslot", 0)])
            DMA("sp", lambda e: e.dma_start(out=dim_, in_=fint[:, 1, :]), ("ost", 1), r=["fint"], w=[("ostslot", 1)])

        chk('m_s5fin')
        gf = lambda kc: g5T.ap[:, kc, :NT]
        gk = lambda kc: g5T.keys(kc * 1024, kc * 1024 + 1024)

        def cons_glub(ci, ps, pk):
            i = ci - 16
            A("act", lambda e: e.activation(out=sgb.ap[:, i, :NT], in_=ps, func=AF.Sigmoid), r=[pk], w=sgb.keys(i * 1024, i * 1024 + 1024))

        def cons_glua(ci, ps, pk):
            i = ci
            A("dve", lambda e: e.tensor_tensor(out=tmpf.ap[:, :NT], in0=ps, in1=sgb.ap[:, i, :NT], op=ALU.mult), r=[pk] + sgb.keys(i * 1024, i * 1024 + 1024), w=tmpf.keys())
            A("dve", lambda e: e.tensor_tensor(out=mB.ap[:, i, :NT], in0=tmpf.ap[:, :NT], in1=mB.ap[:, i, :NT], op=ALU.mult),
              r=tmpf.keys() + mB.keys(i * 1024, i * 1024 + 1024), w=mB.keys(i * 1024, i * 1024 + 1024))

        for s in range(4):
            proj("s5_w_glu", l, 2048 + s * 512, 512, 8, gf, gk, NT, cons_glub)
        for s in range(4):
            proj("s5_w_glu", l, s * 512, 512, 8, gf, gk, NT, cons_glua)

        chk('m_glu')
        SCALE = 128.0 ** -0.5
        fa_alt = [fa, fa2]

        def cons_f(ci, ps, pk):
            hh = ci - 16
            fa = fa_alt[hh % 2]
            kq = qe.keys(hh * 1024, hh * 1024 + 1024)
            kk_ = ke.keys(hh * 1024, hh * 1024 + 1024)
            kl_ = kl.keys(hh * 1024, hh * 1024 + 1024)
            lbc = lbv[:, l, hh:hh + 1]
            omc = omlv[:, l, hh:hh + 1]
            A("act", lambda e: e.activation(out=fa.ap[:, :NT], in_=ps, func=AF.Sigmoid), r=[pk], w=fa.keys())
            A("dve", lambda e: e.tensor_scalar(out=fa.ap[:, :NT], in0=fa.ap[:, :NT], scalar1=omc, scalar2=lbc, op0=ALU.mult, op1=ALU.add), r=fa.keys() + ["omlv", "lbv0", "lbv1"], w=fa.keys())
            A("act", lambda e: e.activation(out=lg.ap[:, :NT], in_=fa.ap[:, :NT], func=AF.Ln), r=fa.keys(), w=lg.keys())
            A("dve", lambda e: e.tensor_tensor_scan(out=bcs.ap[:, :NT], data0=scanm[:, :NT], data1=lg.ap[:, :NT], initial=0.0, op0=ALU.mult, op1=ALU.add),
              r=lg.keys() + ["scanm"], w=bcs.keys())
            A("act", lambda e: e.activation(out=ebb.ap[:, :NT], in_=bcs.ap[:, :NT], func=AF.Exp), r=bcs.keys(), w=ebb.keys())
            A("act", lambda e: e.activation(out=enb.ap[:, :NT], in_=bcs.ap[:, :NT], func=AF.Exp, scale=-1.0), r=bcs.keys(), w=enb.keys())
            ts(fa.ap[:, :NT], fa.ap[:, :NT], -1.0, 1.0, ALU.mult, ALU.add, fa.keys() + lg.keys(), fa.keys())
            A("dve", lambda e: e.scalar_tensor_tensor(out=qe.ap[:, hh, :NT], in0=qe.ap[:, hh, :NT], scalar=SCALE, in1=ebb.ap[:, :NT], op0=ALU.mult, op1=ALU.mult),
              r=kq + ebb.keys(), w=kq)
            tt(ke.ap[:, hh, :NT], fa.ap[:, :NT], enb.ap[:, :NT], ALU.mult, fa.keys() + enb.keys(), kk_)
            nblk = NT // T
            A("dve", lambda e: e.tensor_copy(ebl[:, hh, 0:nblk], ebb.ap[:, :NT].rearrange("p (b t) -> p b t", t=T)[:, :, T - 1]), r=ebb.keys(), w=[("ebl", hh)])
            tt(kl.ap[:, hh, :NT].rearrange("p (b t) -> p b t", t=T), ke.ap[:, hh, :NT].rearrange("p (b t) -> p b t", t=T),
               ebl[:, hh, 0:nblk].unsqueeze(2).broadcast_to([128, nblk, T]), ALU.mult, kk_ + [("ebl", hh)], kl_)

        def cons_g(ci, ps, pk):
            hh = ci - 32
            A("act", lambda e: e.activation(out=sg.ap[:, hh, :NT], in_=ps, func=AF.Silu), r=[pk], w=sg.keys(hh * 1024, hh * 1024 + 1024))

        def v_slab(s):
            slot, view = load_slab("w_in", l, 3072 + s * 256, 256, 16)
            for t_ in range(TT):
                b = next_bank()
                ps = pbanks[b]
                for kc in range(16):
                    A("pe", lambda e, ps=ps, kc=kc, t_=t_: e.matmul(ps[:NTT, 0:256], lhsT=uT[:, kc, t_ * 128:t_ * 128 + NTT], rhs=view[:, kc, :], start=(kc == 0), stop=(kc == 15)),
                      r=[("slab", slot), ("uT", kc)], w=[("pb", b)])
                A("act", lambda e, ps=ps, t_=t_, s=s: e.activation(out=vtok.ap[:NTT, t_, s * 256:(s + 1) * 256], in_=ps[:NTT, 0:256], func=AF.Copy),
                  r=[("pb", b)], w=vtok.keys(t_ * 2048, t_ * 2048 + 2048))

        for s in range(4):
            proj("w_in", l, 2048 + s * 256, 256, 16, rf, rk, NT, cons_f)
            proj("w_in", l, 4096 + s * 256, 256, 16, rf, rk, NT, cons_g)
            v_slab(s)
        chk('m_hgprep')
        def rec_gen(t_, hg_):
            tc0 = t_ * 128
            H4 = [4 * hg_ + i for i in range(4)]
            bAT = 2 + hg_
            bO = 4 + hg_
            bU = 6 if hg_ == 0 else 0
            kt0 = hg_ * 512
            AmT_, klm_, sqh_, rstd_, otmp_ = (AmT, klm, sqh, rstd, otmp) if hg_ == 0 else (AmT2, klm2, sqh2, rstd2, otmp2)
            KPBF = ("pbf", hg_)
            for i, hh in enumerate(H4):
                A("pe", lambda e, i=i, hh=hh: e.matmul(pbanks[bAT][:NTT, i * 128:i * 128 + NTT], lhsT=ke.ap[:, hh, tc0:tc0 + NTT], rhs=qe.ap[:, hh, tc0:tc0 + NTT], start=True, stop=True),
                  r=ke.keys(hh * 1024, hh * 1024 + 1024) + qe.keys(hh * 1024, hh * 1024 + 1024), w=[("pb", bAT)])
                A("pe", lambda e, i=i, hh=hh: e.transpose(pbf[:NTT, kt0 + i * 128:kt0 + i * 128 + 128], kl.ap[:, hh, tc0:tc0 + NTT], identb[:]),
                  r=kl.keys(hh * 1024, hh * 1024 + 1024) + ["identb"], w=[KPBF])
            yield
            A("dve", lambda e: e.tensor_tensor(out=AmT_.ap[:NTT, :, :NTT], in0=pbanks[bAT][:NTT, :].rearrange("p (a b) -> p a b", b=128)[:, :, :NTT],
                                               in1=mask[:NTT, :NTT].unsqueeze(1).broadcast_to([NTT, 4, NTT]), op=ALU.mult),
              r=[("pb", bAT), "mask"], w=AmT_.keys())
            for jb in range(NB):
                bcol = 4 if is_sample else jb
                A("act", lambda e, jb=jb, bcol=bcol: e.activation(out=klm_.ap[:NTT, jb].rearrange("p a b -> p (a b)"), in_=pbf[:NTT, kt0:kt0 + 512], func=AF.Copy, scale=blkm[:NTT, bcol:bcol + 1]),
                  r=[KPBF, "blkm"], w=klm_.keys(jb * 1024, jb * 1024 + 1024))
            yield
            for i, hh in enumerate(H4):
                A("pe", lambda e, i=i, hh=hh: e.matmul(pbanks[bO][:, i * 128:i * 128 + NTT], lhsT=vtok.ap[:NTT, t_, hh * 128:(hh + 1) * 128], rhs=AmT_.ap[:NTT, i, :NTT],
                                                       start=(i == 0), stop=False, skip_group_check=True),
                  r=vtok.keys(t_ * 2048, t_ * 2048 + 2048) + AmT_.keys(), w=[("pb", bO)])
            for jb in range(NB):
                blk = t_ * NB + jb
                for i, hh in enumerate(H4):
                    A("pe", lambda e, i=i, hh=hh, jb=jb: e.matmul(pbanks[bO][:, i * 128 + jb * T:i * 128 + (jb + 1) * T], lhsT=Sbf[:, l, hh, :],
                                                                 rhs=qe.ap[:, hh, tc0 + jb * T:tc0 + (jb + 1) * T], start=False, stop=(jb == NB - 1 and i == 3), skip_group_check=True),
                      r=[("Sbf", l, hh)] + qe.keys(hh * 1024, hh * 1024 + 1024), w=[("pb", bO)])
                for i, hh in enumerate(H4):
                    A("pe", lambda e, i=i, hh=hh, jb=jb: e.matmul(pbanks[bU][:, i * 128:(i + 1) * 128], lhsT=klm_.ap[:NTT, jb, i, :], rhs=vtok.ap[:NTT, t_, hh * 128:(hh + 1) * 128],
                                                                 start=True, stop=True),
                      r=klm_.keys(jb * 1024, jb * 1024 + 1024) + vtok.keys(t_ * 2048, t_ * 2048 + 2048), w=[("pb", bU)])
                yield
                for i, hh in enumerate(H4):
                    A("dve", lambda e, i=i, hh=hh, blk=blk: e.scalar_tensor_tensor(out=Sf[:, l, hh, :], in0=Sf[:, l, hh, :], scalar=ebl[:, hh, blk:blk + 1],
                                                                                  in1=pbanks[bU][:, i * 128:(i + 1) * 128], op0=ALU.mult, op1=ALU.add),
                      r=[("Sf", l, hh), ("ebl", hh), ("pb", bU)], w=[("Sf", l, hh)])
                    A("act", lambda e, hh=hh: e.activation(out=Sbf[:, l, hh, :], in_=Sf[:, l, hh, :], func=AF.Copy), r=[("Sf", l, hh)], w=[("Sbf", l, hh)])
                yield
            o4 = pbanks[bO][:, :].rearrange("p (a b) -> p a b", b=128)[:, :, :NTT]
            A("act", lambda e: e.activation(out=sqh_.ap[:, :, :NTT], in_=o4, func=AF.Square), r=[("pb", bO)], w=sqh_.keys())
            for i in range(4):
                A("pe", lambda e, i=i: e.matmul(pbanks[bAT][:, i * 128:i * 128 + NTT], lhsT=onesb[:], rhs=sqh_.ap[:, i, :NTT], start=True, stop=True),
                  r=sqh_.keys() + ["onesb"], w=[("pb", bAT)])
            yield
            n4 = pbanks[bAT][:, :].rearrange("p (a b) -> p a b", b=128)[:, :, :NTT]
            A("act", lambda e: e.activation(out=rstd_.ap[:, :, :NTT], in_=n4, func=AF.Sqrt, scale=1.0 / 128, bias=epsc[:, 0:1]), r=[("pb", bAT), "epsc"], w=rstd_.keys())
            A("dve", lambda e: e.reciprocal(rstd_.ap[:, :, :NTT], rstd_.ap[:, :, :NTT]), r=rstd_.keys(), w=rstd_.keys())
            tt(otmp_.ap[:, :, :NTT], o4, rstd_.ap[:, :, :NTT], ALU.mult, [("pb", bO)] + rstd_.keys(), otmp_.keys())
            A("dve", lambda e: e.scalar_tensor_tensor(out=onT.ap[:, 4 * hg_:4 * hg_ + 4, tc0:tc0 + NTT], in0=otmp_.ap[:, :, :NTT], scalar=vecT[:, C_HGN(l):C_HGN(l) + 1],
                                                      in1=sg.ap[:, 4 * hg_:4 * hg_ + 4, tc0:tc0 + NTT], op0=ALU.mult, op1=ALU.mult),
              r=otmp_.keys() + ["vecT"] + sg.keys(), w=onT.keys(4096 * hg_, 4096 * hg_ + 4096))

        def rec_tile(t_):
            gens = [rec_gen(t_, 0), rec_gen(t_, 1)]
            alive = [True, True]
            next(gens[0])
            while any(alive):
                for gi_ in range(2):
                    if alive[gi_]:
                        try:
                            next(gens[gi_])
                        except StopIteration:
                            alive[gi_] = False

        for t_ in range(TT):
            rec_tile(t_)
        if last:
            dst = o_shg[l] if is_sample else o_phg[l, seq_slot]
            DMA("sp", lambda e: e.dma_start(out=dst.rearrange("h k v -> k h v"), in_=Sf[:, l]), ("ost", 2), r=[("Sf", l, hh) for hh in range(8)], w=[("ostslot", 2)])

        chk('m_hgrec')
        def cons_yb(ci, ps, pk):
            i = ci
            A("dve", lambda e: e.tensor_tensor(out=tmpf.ap[:, :NT], in0=ps, in1=sgb2[:, i, :NT], op=ALU.mult), r=[pk, ("sgb2", i)], w=tmpf.keys())
            A("dve", lambda e: e.tensor_tensor(out=mB.ap[:, i, :NT], in0=tmpf.ap[:, :NT], in1=mB.ap[:, i, :NT], op=ALU.add),
              r=tmpf.keys() + mB.keys(i * 1024, i * 1024 + 1024), w=mB.keys(i * 1024, i * 1024 + 1024))

        for s in range(4):
            proj("hg_w_out", l, s * 512, 512, 8, lambda kc: onT.ap[:, kc, :NT], lambda kc: onT.keys(kc * 1024, kc * 1024 + 1024), NT, cons_yb)

        def cons_wo(ci, ps, pk):
            A("dve", lambda e: e.tensor_tensor(out=h[:, ci, :NT], in0=ps, in1=h[:, ci, :NT], op=ALU.add), r=[pk, ("h", ci)], w=[("h", ci)])
            stats_chunk(ci, NT)

        for s in range(8):
            proj("w_out", l, s * 256, 256, 16, lambda kc: mB.ap[:, kc, :NT], lambda kc: mB.keys(kc * 1024, kc * 1024 + 1024), NT, cons_wo)

    io_ctr = [0]

    def group_loop():
      for gi, g in enumerate(groups):
          do_group(gi, g)

    def do_group(gi, g):
      if True:
          kind, seq_slot, tile_i, first, last = g
          is_sample = kind == "s"
          NT = 128 if is_sample else 512
          TT = max(1, NT // 128)
          NTT = min(128, NT)
          NV_ = 16 if is_sample else 128
          for t_ in range(TT):
              st = xst[io_ctr[0] % 2]
              sk = ("io", io_ctr[0] % 2)
              io_ctr[0] += 1
              src = xs[:, :] if is_sample else xp[seq_slot, tile_i * 512 + t_ * 128:tile_i * 512 + t_ * 128 + 128, :]
              if is_sample:
                  A("dve", lambda e, st=st: e.memset(st.ap[:, :], 0.0), w=st.keys())
              DMA("sp", lambda e, st=st, src=src: e.dma_start(out=st.ap[:NV_, :], in_=src), sk, w=st.keys())
              for c4 in range(4):
                  b = next_bank()
                  ps = pbanks[b]
                  for k_ in range(4):
                      c = 4 * c4 + k_
                      A("pe", lambda e, ps=ps, k_=k_, c=c, st=st: e.transpose(ps[:, k_ * 128:k_ * 128 + NTT], st.ap[:NTT, c * 128:(c + 1) * 128], ident[:NTT, :NTT]),
                        r=st.keys() + ["ident"], w=[("pb", b)])
                  A("act", lambda e, ps=ps, c4=c4, t_=t_: e.activation(out=h[:, 4 * c4:4 * c4 + 4, t_ * 128:t_ * 128 + NTT],
                                                                       in_=ps[:, :].rearrange("p (a b) -> p a b", b=128)[:, :, :NTT], func=AF.Copy),
                    r=[("pb", b)], w=[("h", 4 * c4 + k_) for k_ in range(4)])
          for c in range(DC):
              stats_chunk(c, NT)
          chk("xload")
          for l in range(n_layers):
              ffn(l, 1, NT)
              chk("ffn1_%d" % l)
              mixer(l, NT, gi, first, last, is_sample, seq_slot)
              chk("mixer_%d" % l)
              ffn(l, 2, NT)
              chk("ffn2_%d" % l)
          norm_finish(NT, C_NFIN, lambda c: yfull.ap[:, c, :NT], lambda c: yfull.keys(c * 2048, c * 2048 + 2048))
          for t_ in range(TT):
              st = xst[io_ctr[0] % 2]
              sk = ("io", io_ctr[0] % 2)
              io_ctr[0] += 1
              for c4 in range(4):
                  b = next_bank()
                  ps = pbanks[b]
                  for k_ in range(4):
                      c = 4 * c4 + k_
                      A("pe", lambda e, ps=ps, k_=k_, c=c, t_=t_: e.transpose(ps[:NTT, k_ * 128:(k_ + 1) * 128], yfull.ap[:, c, t_ * 128:t_ * 128 + NTT], ident[:]),
                        r=yfull.keys(c * 2048, c * 2048 + 2048) + ["ident"], w=[("pb", b)])
                  A("act", lambda e, ps=ps, c4=c4, st=st: e.activation(out=st.ap[:NTT, c4 * 512:(c4 + 1) * 512], in_=ps[:NTT, :], func=AF.Copy),
                    r=[("pb", b)], w=st.keys())
              dst = ys[:, :] if is_sample else yp[seq_slot, tile_i * 512 + t_ * 128:tile_i * 512 + t_ * 128 + 128, :]
              DMA("sp", lambda e, st=st, dst=dst: e.dma_start(out=dst, in_=st.ap[:NV_, :]), sk, r=st.keys(), w=[("iodone", sk)])

    try:
        body()
    except StopBuild as ex:
        print("DBG stop at", ex)
    print('sbuf bytes remaining', nc.sbuf_bytes_remaining)
    final_keys = [("io", 0), ("io", 1), ("ost", 0), ("ost", 1), ("ost", 2)]
    S.emit(nc, final_keys)
    es.close()
    return nc


def _groups_full():
    gs = []
    for sq in range(2):
        for t in range(4):
            gs.append(("p", sq, t, t == 0, t == 3))
    gs.append(("s", 0, 0, True, True))
    return gs


def _host_layout(inputs, c):
    f32 = np.float32
    d = {}
    d["xp"] = np.ascontiguousarray(inputs["x_prompt"][2 * c:2 * c + 2]).astype(f32, copy=False)
    d["xs"] = np.ascontiguousarray(inputs["x_sample"][c]).astype(f32, copy=False)
    def sm(a):
        return a.reshape(32, 2, 64).transpose(1, 2, 0).reshape(128, 32)
    sx0 = np.stack([np.stack([sm(inputs["state_s5_re"][l, c]), sm(inputs["state_s5_im"][l, c])], axis=-1) for l in range(2)], axis=0)
    d["sx0"] = np.ascontiguousarray(sx0, dtype=f32)
    d["shg"] = np.ascontiguousarray(inputs["state_hgrn"][:, c], dtype=f32)
    return d


def _shared_layout(inputs):
    f32 = np.float32
    d = {}
    cols = []
    def fm(v):
        return np.asarray(v, dtype=f32).reshape(-1, 128).T
    for l in range(2):
        cols += [fm(inputs["norm_ffn1"][l]), fm(inputs["norm_mix"][l]), fm(inputs["norm_ffn2"][l])]
    cols.append(fm(inputs["norm_final"]))
    for l in range(2):
        cols.append(fm(inputs["s5_d"][l]))
    for l in range(2):
        cols.append(fm(inputs["hg_lb"][l]))
    for l in range(2):
        cols.append(fm(inputs["hg_norm"][l]))
    d["vecT"] = np.ascontiguousarray(np.concatenate(cols, axis=1), dtype=f32)
    assert d["vecT"].shape == (128, NV)
    s5p = np.zeros((2, 128, 32, 67), f32)
    for l in range(2):
        def sm(a):
            return np.asarray(a).reshape(32, 2, 64).transpose(1, 2, 0).reshape(128, 32)
        s5p[l, :, :, 0] = sm(inputs["s5_lam_re"][l])
        s5p[l, :, :, 1] = sm(inputs["s5_lam_im"][l])
        s5p[l, :, :, 2] = sm(np.repeat(np.asarray(inputs["s5_log_step"][l])[:, None], 64, axis=1))
        def smb(a):
            return np.asarray(a).reshape(32, 2, 64, 16).transpose(1, 2, 0, 3).reshape(128, 32, 16)
        s5p[l, :, :, 3:19] = smb(inputs["s5_b_re"][l])
        s5p[l, :, :, 19:35] = smb(inputs["s5_b_im"][l])
        def smc(a):
            return np.asarray(a).reshape(32, 2, 16, 64).transpose(1, 3, 0, 2).reshape(128, 32, 16)
        s5p[l, :, :, 35:51] = smc(inputs["s5_c_re"][l])
        s5p[l, :, :, 51:67] = smc(inputs["s5_c_im"][l])
    d["s5p"] = s5p
    d["ident"] = np.eye(128, dtype=f32)
    s = np.arange(128)[:, None]
    t = np.arange(128)[None, :]
    d["mask"] = ((s <= t) & (s // 32 == t // 32)).astype(f32)
    sc = np.ones((128, 512), f32)
    sc[:, 0::32] = 0.0
    d["scanm"] = sc
    bm = np.zeros((128, 5), f32)
    bm[:, 0:4] = (np.arange(128)[:, None] // 32 == np.arange(4)[None, :])
    bm[0:16, 4] = 1.0
    d["blkm"] = bm
    for n in WSPEC:
        d[n] = np.ascontiguousarray(inputs[n], dtype=f32)
    return d


_PROG = {}


def kernel(**inputs):
    groups = _groups_full()
    key = "full"
    if key not in _PROG:
        _PROG[key] = build_program(groups)
    nc = _PROG[key]
    shared = _shared_layout(inputs)
    in_maps = []
    for c in range(NCORES):
        m = dict(shared)
        m.update(_host_layout(inputs, c))
        in_maps.append(m)
    res = run_bass_kernel_spmd(nc, in_maps, core_ids=list(range(NCORES))).results
    f32 = np.float32
    y_prompt = np.concatenate([r["yp"] for r in res], axis=0).astype(f32, copy=False)
    y_sample = np.stack([r["ys"] for r in res], axis=0).astype(f32, copy=False)
    def s5out(name_p):
        a = np.stack([r[name_p] for r in res], axis=1)
        return np.ascontiguousarray(a.reshape(2, 16, 64, 64), dtype=f32)
    p_re = s5out("o_pre")
    p_im = s5out("o_pim")
    p_hg = np.ascontiguousarray(np.stack([r["o_phg"] for r in res], axis=1).reshape(2, 16, 8, 128, 128), dtype=f32)
    s_re = np.ascontiguousarray(np.stack([r["o_sre"] for r in res], axis=1).reshape(2, 8, 64, 64), dtype=f32)
    s_im = np.ascontiguousarray(np.stack([r["o_sim"] for r in res], axis=1).reshape(2, 8, 64, 64), dtype=f32)
    s_hg = np.ascontiguousarray(np.stack([r["o_shg"] for r in res], axis=1), dtype=f32)
    return (y_prompt, y_sample, p_re, p_im, p_hg, s_re, s_im, s_hg)
```
